# Optimizing a Trainium2 kernel written in Bass

```python
import jax, jax.numpy as jnp
from jax import lax
import numpy as np

D_MODEL = 1024
BATCH = 8
SEQ = 2048
DEPTH = 4

N_MIXERS = 3
N_A_LAYERS = (DEPTH + 2) // 3
N_B_LAYERS = (DEPTH + 1) // 3
N_C_LAYERS = DEPTH // 3

A_CHUNK = 128
A_GROUPS = 8
A_WIDTH = 3 * D_MODEL
A_GROUP_DIM = A_WIDTH // A_GROUPS
B_WIDTH = D_MODEL
B_WINDOWS = (2, 4, 8, 16)
B_GROUPS = len(B_WINDOWS)
B_GROUP_DIM = B_WIDTH // B_GROUPS
C_WIDTH = D_MODEL
C_KERNEL = 31
MLP_HIDDEN = 4 * D_MODEL
EPS = 1e-6

kernel_name = "interleaved_gmlp_pool_conformer_trunk"


def rmsnorm(x, g):
    xf = x.astype(jnp.float32)
    y = xf * lax.rsqrt(jnp.mean(xf * xf, axis=-1, keepdims=True) + EPS)
    return (y * g.astype(jnp.float32)).astype(x.dtype)


def layernorm(x, g, b):
    xf = x.astype(jnp.float32)
    mu = jnp.mean(xf, axis=-1, keepdims=True)
    var = jnp.mean(jnp.square(xf - mu), axis=-1, keepdims=True)
    y = (xf - mu) * lax.rsqrt(var + EPS)
    return (y * g.astype(jnp.float32) + b.astype(jnp.float32)).astype(x.dtype)


def mixer_a(h, w_in, b_in, vn_g, vn_b, w_s, b_s, w_out):
    bsz, s, _ = h.shape
    z = jax.nn.gelu(h @ w_in + b_in)
    u, v = jnp.split(z, 2, axis=-1)
    v = layernorm(v, vn_g, vn_b)
    v = v.reshape(bsz, s // A_CHUNK, A_CHUNK, A_GROUPS, A_GROUP_DIM)
    causal = jnp.tril(jnp.ones((A_CHUNK, A_CHUNK), dtype=bool))
    w = jnp.where(causal[None], w_s, jnp.zeros_like(w_s))
    gate = jnp.einsum('gts,bcsgd->bctgd', w, v) + b_s.T[None, None, :, :, None]
    gate = gate.reshape(bsz, s, A_WIDTH)
    return (u * gate) @ w_out


def mixer_b(h, w_in, w_grp, b_grp, scale, w_out):
    bsz, s, _ = h.shape
    p = (h @ w_in).astype(jnp.float32)
    cs0 = jnp.pad(jnp.cumsum(p, axis=1), ((0, 0), (1, 0), (0, 0)))
    pos = jnp.arange(s)
    groups = []
    for g, win in enumerate(B_WINDOWS):
        c = cs0[:, :, g * B_GROUP_DIM:(g + 1) * B_GROUP_DIM]
        lagged = jnp.pad(c, ((0, 0), (win - 1, 0), (0, 0)))[:, :s]
        win_sum = c[:, 1:] - lagged
        count = jnp.minimum(pos + 1, win).astype(jnp.float32)
        mean = win_sum / count[None, :, None]
        groups.append(mean - p[:, :, g * B_GROUP_DIM:(g + 1) * B_GROUP_DIM])
    pooled = jnp.stack(groups, axis=2).astype(h.dtype)
    y = jnp.einsum('bsgd,gde->bsge', pooled, w_grp) + b_grp
    y = y.reshape(bsz, s, B_WIDTH) * scale
    return y @ w_out


def mixer_c(h, w_in, w_dw, b_dw, ln_g, ln_b, w_out):
    a, g = jnp.split(h @ w_in, 2, axis=-1)
    y = a * jax.nn.sigmoid(g)
    y = lax.conv_general_dilated(
        y, w_dw[:, None, :].astype(y.dtype), window_strides=(1,),
        padding=[(C_KERNEL - 1, 0)],
        dimension_numbers=('NWC', 'WIO', 'NWC'),
        feature_group_count=C_WIDTH) + b_dw
    y = jax.nn.silu(layernorm(y, ln_g, ln_b))
    return y @ w_out


def channel_mlp(h, w1, w2):
    return jnp.square(jax.nn.relu(h @ w1)) @ w2


def setup_inputs(seed: int = 0) -> dict:
    key = jax.random.key(seed)
    ks = iter(jax.random.split(key, 32))

    def nrm(shape, scale):
        return jax.random.normal(next(ks), shape, jnp.float32) * scale

    def gain(shape, noise=0.05):
        return 1.0 + nrm(shape, noise)

    na, nb, nc = N_A_LAYERS, N_B_LAYERS, N_C_LAYERS
    return {
        "x": nrm((BATCH, SEQ, D_MODEL), 1.0),
        "norm_mix": gain((DEPTH, D_MODEL)),
        "norm_mlp": gain((DEPTH, D_MODEL)),
        "norm_final": gain((D_MODEL,)),
        "a_w_in": nrm((na, D_MODEL, 2 * A_WIDTH), D_MODEL ** -0.5),
        "a_b_in": nrm((na, 2 * A_WIDTH), 0.01),
        "a_vn_g": gain((na, A_WIDTH)),
        "a_vn_b": nrm((na, A_WIDTH), 0.01),
        "a_w_s": nrm((na, A_GROUPS, A_CHUNK, A_CHUNK), A_CHUNK ** -0.5),
        "a_b_s": gain((na, A_GROUPS, A_CHUNK), 0.1),
        "a_w_out": nrm((na, A_WIDTH, D_MODEL), A_WIDTH ** -0.5),
        "b_w_in": nrm((nb, D_MODEL, B_WIDTH), D_MODEL ** -0.5),
        "b_w_grp": nrm((nb, B_GROUPS, B_GROUP_DIM, B_GROUP_DIM), B_GROUP_DIM ** -0.5),
        "b_b_grp": nrm((nb, B_GROUPS, B_GROUP_DIM), 0.01),
        "b_scale": gain((nb, B_WIDTH), 0.1),
        "b_w_out": nrm((nb, B_WIDTH, D_MODEL), B_WIDTH ** -0.5),
        "c_w_in": nrm((nc, D_MODEL, 2 * C_WIDTH), D_MODEL ** -0.5),
        "c_w_dw": nrm((nc, C_KERNEL, C_WIDTH), C_KERNEL ** -0.5),
        "c_b_dw": nrm((nc, C_WIDTH), 0.01),
        "c_ln_g": gain((nc, C_WIDTH)),
        "c_ln_b": nrm((nc, C_WIDTH), 0.01),
        "c_w_out": nrm((nc, C_WIDTH, D_MODEL), C_WIDTH ** -0.5),
        "m_w1": nrm((DEPTH, D_MODEL, MLP_HIDDEN), D_MODEL ** -0.5),
        "m_w2": nrm((DEPTH, MLP_HIDDEN, D_MODEL), MLP_HIDDEN ** -0.5),
    }


def reference(x, norm_mix, norm_mlp, norm_final,
              a_w_in, a_b_in, a_vn_g, a_vn_b, a_w_s, a_b_s, a_w_out,
              b_w_in, b_w_grp, b_b_grp, b_scale, b_w_out,
              c_w_in, c_w_dw, c_b_dw, c_ln_g, c_ln_b, c_w_out,
              m_w1, m_w2):
    ia = ib = ic = 0
    for i in range(DEPTH):
        h = rmsnorm(x, norm_mix[i])
        kind = i % N_MIXERS
        if kind == 0:
            y = mixer_a(h, a_w_in[ia], a_b_in[ia], a_vn_g[ia], a_vn_b[ia],
                        a_w_s[ia], a_b_s[ia], a_w_out[ia])
            ia += 1
        elif kind == 1:
            y = mixer_b(h, b_w_in[ib], b_w_grp[ib], b_b_grp[ib], b_scale[ib], b_w_out[ib])
            ib += 1
        else:
            y = mixer_c(h, c_w_in[ic], c_w_dw[ic], c_b_dw[ic], c_ln_g[ic], c_ln_b[ic], c_w_out[ic])
            ic += 1
        x = x + y
        x = x + channel_mlp(rmsnorm(x, norm_mlp[i]), m_w1[i], m_w2[i])
    return rmsnorm(x, norm_final)
```

```python
import numpy as np
from contextlib import ExitStack

import concourse.bass as bass
import concourse.mybir as mybir
from concourse.bass_utils import run_bass_kernel_spmd

F32 = mybir.dt.float32
BF16 = mybir.dt.bfloat16
AF = mybir.ActivationFunctionType
ALU = mybir.AluOpType

D = 1024
KC = 8
AW = 3072
ACJ = 24
HID = 4096
EPS = 1e-6
B_WINDOWS = (2, 4, 8, 16)
CK = 31
NSLOT = 5
NSQ = 6
SLOT_ELEMS = 4096


def _pvec_layout():
    cols = {}
    n = 0

    def add(name, ncols):
        nonlocal n
        cols[name] = n
        n += ncols

    for l in range(4):
        add(f"norm_mix{l}", 8)
        add(f"norm_mlp{l}", 8)
    add("norm_final", 8)
    for ia in range(2):
        add(f"a_bu{ia}", ACJ)
        add(f"a_vng{ia}", ACJ)
    add("b_bgrp", 8)
    add("b_scale", 8)
    add("c_wdw", CK * 8)
    add("c_bdw", 8)
    add("c_lng", 8)
    add("c_lnb", 8)
    return cols, n


PCOLS, NPCOL = _pvec_layout()


def _cols(v):
    v = np.asarray(v, dtype=np.float32).reshape(-1, 128)
    return np.ascontiguousarray(v.T)


def build_pvec(inp):
    pv = np.zeros((128, NPCOL), dtype=np.float32)

    def put(name, v):
        c = _cols(v)
        pv[:, PCOLS[name]:PCOLS[name] + c.shape[1]] = c

    for l in range(4):
        put(f"norm_mix{l}", inp["norm_mix"][l])
        put(f"norm_mlp{l}", inp["norm_mlp"][l])
    put("norm_final", inp["norm_final"])
    for ia in range(2):
        put(f"a_bu{ia}", inp["a_b_in"][ia][:AW])
        put(f"a_vng{ia}", inp["a_vn_g"][ia])
    put("b_bgrp", np.asarray(inp["b_b_grp"][0]).reshape(-1))
    put("b_scale", inp["b_scale"][0])
    wdw = np.asarray(inp["c_w_dw"][0], dtype=np.float32).reshape(CK, 8, 128)
    pv[:, PCOLS["c_wdw"]:PCOLS["c_wdw"] + CK * 8] = wdw.transpose(2, 1, 0).reshape(128, 8 * CK)
    put("c_bdw", inp["c_b_dw"][0])
    put("c_lng", inp["c_ln_g"][0])
    put("c_lnb", inp["c_ln_b"][0])
    return pv


def I(_method, *args, **kwargs):
    return lambda e: getattr(e, _method)(*args, **kwargs)


class Res:
    __slots__ = ("name", "w", "r")
    init_w = {}

    def __init__(self, name):
        self.name = name
        self.w = dict(Res.init_w)
        self.r = {}


class Sched:
    ENGS = ("pe", "act", "dve", "pool", "sp")

    def __init__(self, nc, es):
        self.nc = nc
        self.es = es
        self.q = {e: [] for e in self.ENGS}
        self.sems = {}
        self.cnt = {}
        self.waited = {}
        for e in self.ENGS:
            self._mksem("E_" + e)

    def _mksem(self, key):
        h = self.es.enter_context(self.nc.semaphore(key))
        self.sems[key] = h
        self.cnt[key] = 0
        return key

    def dma_sem(self, name):
        return self._mksem("D_" + name)

    def _wait(self, eng, key, val):
        if eng == "pe" and key == "E_pe":
            return
        if self.waited.get((eng, key), 0) >= val:
            return
        self.waited[(eng, key)] = val
        self.q[eng].append(("w", key, val))

    def op(self, eng, fns, rd=(), wr=(), dma=None):
        if not isinstance(fns, (list, tuple)):
            fns = [fns]
        deps = {}
        for r in rd:
            for k, v in r.w.items():
                if deps.get(k, 0) < v:
                    deps[k] = v
        for r in wr:
            for k, v in r.w.items():
                if deps.get(k, 0) < v:
                    deps[k] = v
            for k, v in r.r.items():
                if deps.get(k, 0) < v:
                    deps[k] = v
        for k, v in deps.items():
            self._wait(eng, k, v)
        if dma is None:
            key, inc = "E_" + eng, 1
        else:
            key, inc = dma, 16
        self.cnt[key] += inc
        val = self.cnt[key]
        for f in fns[:-1]:
            self.q[eng].append(("i", f, None, 0))
        self.q[eng].append(("i", fns[-1], key, inc))
        for r in rd:
            if r.r.get(key, 0) < val:
                r.r[key] = val
        for r in wr:
            r.w = {key: val}
            r.r = {}
        return (key, val)

    def barrier(self, engs=("pe", "act", "dve", "sp")):
        for e in engs:
            for f in engs:
                if e != f and self.cnt["E_" + f] > 0:
                    self._wait(e, "E_" + f, self.cnt["E_" + f])

    def end_phase(self):
        Res.init_w = {"E_" + e: self.cnt["E_" + e] for e in ("pe", "act", "dve", "pool") if self.cnt["E_" + e] > 0}

    def wait_all(self, eng, keys):
        for k in keys:
            if self.cnt[k] > 0:
                self._wait(eng, k, self.cnt[k])

    def emit(self, block):
        def play(eng_name):
            def body(e):
                for it in self.q[eng_name]:
                    if it[0] == "w":
                        e.wait_ge(self.sems[it[1]], it[2])
                    else:
                        ins = it[1](e)
                        if it[2] is not None:
                            ins.then_inc(self.sems[it[2]], it[3])
            return body
        if self.q["sp"]:
            block.sync(play("sp"))
        if self.q["pool"]:
            block.gpsimd(play("pool"))
        if self.q["act"]:
            block.scalar(play("act"))
        if self.q["dve"]:
            block.vector(play("dve"))
        if self.q["pe"]:
            block.tensor(play("pe"))


def build_program(S, layers, final, used_inputs=None):
    assert S % 512 == 0
    NT = S // 512
    NCH = S // 128
    nc = bass.Bass("TRN2", target_bir_lowering=False)
    es = ExitStack()

    kinds = sorted(set(l % 3 for l in layers))
    dram = {}

    def din(name, shape):
        dram[name] = nc.dram_tensor(name, list(shape), F32, kind="ExternalInput").ap()

    din("x", (S, D))
    din("pvec", (128, NPCOL))
    if final:
        din("norm_final", (D,))
    if 0 in kinds:
        din("a_w_in", (2, D, 2 * AW))
        din("a_b_in", (2, 2 * AW))
        din("a_vn_b", (2, AW))
        din("a_w_s", (2, 8, 128, 128))
        din("a_b_s", (2, 8 * 128))
        din("a_w_out", (2, AW, D))
    if 1 in kinds:
        din("b_w_in", (1, D, D))
        din("b_w_grp", (1, 4, 256, 256))
        din("b_w_out", (1, D, D))
    if 2 in kinds:
        din("c_w_in", (1, D, 2 * D))
        din("c_w_out", (1, D, D))
    if layers:
        din("m_w1", (4, D, HID))
        din("m_w2", (4, HID, D))
    out_d = nc.dram_tensor("out", [S, D], F32, kind="ExternalOutput").ap()

    Res.init_w = {}
    sc = Sched(nc, es)

    uid = [0]

    def sb(name, shape, dt, stack=None):
        uid[0] += 1
        return (stack or es).enter_context(nc.sbuf_tensor(f"{name}_{uid[0]}", list(shape), dt))

    xT = sb("xT", (128, KC, S), F32)
    xT_res = [[Res(f"xT{kc}_{tt}") for tt in range(NT)] for kc in range(KC)]
    pv = sb("pv", (128, NPCOL), F32)
    pv_res = Res("pv")
    ident = sb("ident", (128, 128), F32)
    onesf = sb("onesf", (128, 128), F32)
    cmask = sb("cmask", (128, 128), F32)
    invc = sb("invc", (128, 16), F32)
    onesm = sb("onesm", (128, 128), BF16)
    epst = sb("epst", (128, 1), F32)
    const_res = Res("const")
    ring = [sb(f"ring{i}", (128, SLOT_ELEMS), BF16) for i in range(NSLOT)]
    ring_res = [Res(f"ring{i}") for i in range(NSLOT)]
    ring_sem = [sc.dma_sem(f"ring{i}") for i in range(NSLOT)]
    ring_next = [0]

    rb_persist = {
        "sq": [sb(f"rsq{i}", (128, 512), BF16) for i in range(NSQ)],
        "sq_res": [Res(f"rsq{i}") for i in range(NSQ)],
        "sd": [sb(f"rsd{i}", (128, 512), F32) for i in range(2)],
        "sd_res": [Res(f"rsd{i}") for i in range(2)],
        "n": 0,
        "k": 0,
    }

    def next_sq():
        i = rb_persist["k"] % NSQ
        rb_persist["k"] += 1
        return rb_persist["sq"][i], rb_persist["sq_res"][i]

    hTp = [sb(f"hTp{i}", (128, KC, 512), BF16) for i in range(2)]
    hTp_res = [[Res(f"hTp{i}_{kc}") for kc in range(KC)] for i in range(2)]

    banks = [es.enter_context(nc.psum_tensor(f"bank{i}", [128, 512], F32)) for i in range(8)]
    bank_res = [Res(f"bank{i}") for i in range(8)]
    bank_next = [0]

    def next_bank():
        i = bank_next[0]
        bank_next[0] = (i + 1) % 8
        return banks[i], bank_res[i]

    def load_w(src_ap, view):
        i = ring_next[0]
        ring_next[0] = (i + 1) % NSLOT
        a, b = view
        assert a * b <= SLOT_ELEMS
        dst = ring[i][:, 0:a * b].rearrange("p (a b) -> p a b", a=a)
        sc.op("pool", I("dma_start", out=dst, in_=src_ap, max_dma_last_dim=8192),
              wr=[ring_res[i]], dma=ring_sem[i])
        return dst, ring_res[i]

    psem = sc.dma_sem("params")
    sc.op("sp", I("dma_start", out=pv[:], in_=dram["pvec"]), wr=[pv_res], dma=psem)
    sc.op("pool", I("memset", onesf[:], 1.0), wr=[const_res])
    sc.op("pool", I("memset", onesm[:], 1.0 / D), wr=[const_res])
    sc.op("pool", I("memset", epst[:], EPS), wr=[const_res])
    sc.op("pool", I("affine_select", out=ident[:], in_=onesf[:], pattern=[[1, 128]],
                                            compare_op=ALU.is_equal, fill=0.0, base=0, channel_multiplier=-1),
          rd=[const_res], wr=[const_res])
    sc.op("pool", I("affine_select", out=cmask[:], in_=onesf[:], pattern=[[1, 128]],
                    compare_op=ALU.is_ge, fill=0.0, base=0, channel_multiplier=-1),
          rd=[const_res], wr=[const_res])
    invc_res = Res("invc")
    for j in range(16):
        sc.op("pool", I("memset", invc[:, j:j + 1], 1.0 / (j + 1)), wr=[invc_res])

    evac_rr = [0]

    def evac_copy(out_ap, in_ap, rd, wr):
        evac_rr[0] ^= 1
        if evac_rr[0]:
            sc.op("act", I("activation", out=out_ap, in_=in_ap, func=AF.Copy), rd=rd, wr=wr)
        else:
            sc.op("dve", I("tensor_copy", out=out_ap, in_=in_ap), rd=rd, wr=wr)

    def input_phase(stack, after_chunk=None):
        NXB = 4
        xin = [sb(f"xin{i}", (128, D), F32, stack) for i in range(NXB)]
        xin_res = [Res(f"xin{i}") for i in range(NXB)]
        xin_sem = [sc.dma_sem(f"xin{i}") for i in range(NXB)]
        for c in range(NCH):
            b = c % NXB
            tt = c // 4
            sc.op("sp", I("dma_start", out=xin[b][:], in_=dram["x"][c * 128:(c + 1) * 128, :]),
                  wr=[xin_res[b]], dma=xin_sem[b])
            for half in range(2):
                bk, bres = next_bank()
                fns = []
                for j in range(4):
                    dc = half * 4 + j
                    fns.append(I("transpose",
                        out=bk[:, j * 128:(j + 1) * 128], in_=xin[b][:, dc * 128:(dc + 1) * 128], identity=ident[:]))
                sc.op("pe", fns, rd=[xin_res[b], const_res], wr=[bres])
                evac_copy(xT[:, half * 4:half * 4 + 4, c * 128:(c + 1) * 128],
                          bk[:].rearrange("p (a b) -> p a b", a=4),
                          rd=[bres], wr=[xT_res[half * 4 + j][tt] for j in range(4)])
            if after_chunk is not None:
                after_chunk(c)

    def make_rms_bufs(stack):
        d = {
            "sq": [sb(f"rsq{i}", (128, 512), BF16, stack) for i in range(2)],
            "sq_res": [Res(f"rsq{i}") for i in range(2)],
            "sd": [sb(f"rsd{i}", (128, 512), F32, stack) for i in range(2)],
            "sd_res": [Res(f"rsd{i}") for i in range(2)],
            "n": 0,
        }
        return d

    def rms_rstd(rb, tt, dst=None, dst_res=None):
        t0 = tt * 512
        par = rb["n"] % 2
        rb["n"] += 1
        bk, bres = next_bank()
        for kc in range(KC):
            sq, sqr = next_sq()
            sc.op("act", I("activation", out=sq[:], in_=xT[:, kc, t0:t0 + 512], func=AF.Square),
                  rd=[xT_res[kc][tt]], wr=[sqr])
            sc.op("pe", I("matmul", bk[:], lhsT=onesm[:], rhs=sq[:], start=(kc == 0), stop=(kc == KC - 1)),
                  rd=[sqr, const_res], wr=[bres] if kc == 0 else [], )
        if dst is None:
            sd, sdr = rb["sd"][par][:], rb["sd_res"][par]
        else:
            sd, sdr = dst, dst_res
        bres.w = {"E_pe": sc.cnt["E_pe"]}
        sc.op("act", I("activation", out=sd, in_=bk[:], func=AF.Sqrt, bias=epst[:, 0:1], scale=1.0),
              rd=[bres, const_res], wr=[sdr])
        sc.op("dve", I("reciprocal", out=sd, in_=sd), rd=[], wr=[sdr])
        return sd, sdr

    def rmsnorm(rb, tt, gcol, out_fn, out_res_fn):
        t0 = tt * 512
        sd, sdr = rms_rstd(rb, tt)
        for kc in range(KC):
            o = out_fn(kc)
            sc.op("dve", I("scalar_tensor_tensor",
                out=o, in0=xT[:, kc, t0:t0 + 512], scalar=pv[:, gcol + kc:gcol + kc + 1], in1=sd,
                op0=ALU.mult, op1=ALU.mult),
                rd=[xT_res[kc][tt], sdr, pv_res], wr=[out_res_fn(kc)])

    def add_to_x(dc, tt, bk, bres):
        t0 = tt * 512
        sc.op("dve", I("tensor_tensor", out=xT[:, dc, t0:t0 + 512], in0=bk[:], in1=xT[:, dc, t0:t0 + 512], op=ALU.add),
              rd=[bres], wr=[xT_res[dc][tt]])

    def mlp(l, tail_setup=None, next_rn0=None):
        with ExitStack() as ph:
            rb = rb_persist
            tail = tail_setup(ph) if tail_setup is not None else None
            hT = sb("m_hT", (128, KC, S), BF16, ph)
            hT_res = [[Res(f"m_hT{kc}_{tt}") for tt in range(NT)] for kc in range(KC)]
            aT = [sb(f"m_aT{i}", (128, 4, 512), BF16, ph) for i in range(2)]
            aT_res = [[Res(f"m_aT{i}_{fc}") for fc in range(4)] for i in range(2)]
            rl = [sb(f"m_rl{i}", (128, 512), F32, ph) for i in range(2)]
            rl_res = [Res(f"m_rl{i}") for i in range(2)]
            gcol = PCOLS[f"norm_mlp{l}"]
            rstd_b = sb("m_rstd", (128, NT, 512), F32, ph)
            rstd_res = [Res(f"m_rstd{tt}") for tt in range(NT)]
            def prep(tt):
                for kc in range(KC):
                    sc.op("dve", I("tensor_scalar", out=hT[:, kc, tt * 512:(tt + 1) * 512], in0=xT[:, kc, tt * 512:(tt + 1) * 512],
                                   scalar1=pv[:, gcol + kc:gcol + kc + 1], scalar2=None, op0=ALU.mult),
                          rd=[xT_res[kc][tt], pv_res], wr=[hT_res[kc][tt]])
                rms_rstd(rb, tt, dst=rstd_b[:, tt, :], dst_res=rstd_res[tt])
            w1 = dram["m_w1"]
            w2 = dram["m_w2"]
            state = {}
            rlc = [0]

            def W1(i):
                fb, tt = divmod(i, NT)
                if fb == 0:
                    prep(tt)
                if tt == 0:
                    s1 = load_w(w1[l, :, fb * 512:(fb + 1) * 512].rearrange("(kc p) f -> p kc f", p=128), (8, 512))
                    s2 = load_w(w2[l, fb * 512:(fb + 1) * 512, :].rearrange("(fc p) d -> p fc d", p=128), (4, 1024))
                    state[fb] = (s1, s2)
                (wa, war), _ = state[fb]
                ab = i % 2
                for fc in range(4):
                    bk, bres = next_bank()
                    fns = [I("matmul",
                        bk[:], lhsT=wa[:, kc, fc * 128:(fc + 1) * 128], rhs=hT[:, kc, tt * 512:(tt + 1) * 512],
                        start=(kc == 0), stop=(kc == KC - 1)) for kc in range(KC)]
                    sc.op("pe", fns, rd=[war] + [hT_res[kc][tt] for kc in range(KC)], wr=[bres])
                    r = rlc[0] % 2
                    rlc[0] += 1
                    sc.op("dve", I("scalar_tensor_tensor", out=rl[r][:], in0=bk[:], scalar=0.0, in1=rstd_b[:, tt, :],
                                   op0=ALU.max, op1=ALU.mult),
                          rd=[bres, rstd_res[tt]], wr=[rl_res[r]])
                    sc.op("act", I("activation", out=aT[ab][:, fc, :], in_=rl[r][:], func=AF.Square),
                          rd=[rl_res[r]], wr=[aT_res[ab][fc]])

            def W2(i):
                fb, tt = divmod(i, NT)
                _, (wb, wbr) = state[fb]
                ab = i % 2
                for dc in range(KC):
                    bk, bres = next_bank()
                    fns = [I("matmul",
                        bk[:], lhsT=wb[:, fc, dc * 128:(dc + 1) * 128], rhs=aT[ab][:, fc, :],
                        start=(fc == 0), stop=(fc == 3)) for fc in range(4)]
                    sc.op("pe", fns, rd=[wbr] + aT_res[ab], wr=[bres])
                    add_to_x(dc, tt, bk, bres)
                if tail is not None and fb == 7:
                    tail(tt)
                if next_rn0 is not None and fb == 7 and tt == 0:
                    next_rn0()

            n = 8 * NT
            W1(0)
            for i in range(n):
                if i + 1 < n:
                    W1(i + 1)
                W2(i)
            sc.end_phase()

    def mixer_a(l, ia, with_input=False, rn0_done=False):
        gcol = PCOLS[f"norm_mix{l}"]
        bucol = PCOLS[f"a_bu{ia}"]
        vgcol = PCOLS[f"a_vng{ia}"]
        w_in = dram["a_w_in"]
        w_out = dram["a_w_out"]
        with ExitStack() as ph:
            Ssb = sb("a_S", (128, ACJ, 128), F32, ph)
            S_res = Res("a_S")
            wmb = sb("a_wmb", (128, 8, 128), BF16, ph)
            wmb_res = Res("a_wmb")
            brow = sb("a_brow", (128, AW), BF16, ph)
            brow_res = Res("a_brow")
            sel1 = sb("a_sel1", (128, 128), BF16, ph)
            sel1_res = Res("a_sel1")
            asem = sc.dma_sem(f"a_pro{ia}_w")
            asem_b = sc.dma_sem(f"a_pro{ia}_b")
            asem_r = sc.dma_sem(f"a_pro{ia}_r")
            asem_l = sc.dma_sem(f"a_pro{ia}_l")
            def rn0():
                rmsnorm(rb_persist, 0, gcol, lambda kc: hTp[0][:, kc, :], lambda kc: hTp_res[0][kc])

            with ExitStack() as pro:
                def prologue():
                    wst = sb("a_wst", (128, 8, 128), F32, pro)
                    wst_res = Res("a_wst")
                    wmf2 = sb("a_wmf2", (128, 8, 128), F32, pro)
                    wmf2_res = Res("a_wmf2")
                    rrow = sb("a_rrow", (33, 1024), F32, pro)
                    rrow_res = Res("a_rrow")
                    lrow = sb("a_lrow", (33, AW), F32, pro)
                    lrow_res = Res("a_lrow")
                    bsrc = sb("a_bsrc", (1, AW), F32, pro)
                    bsrc_res = Res("a_bsrc")
                    sc.op("sp", I("dma_start", out=wst[:], in_=dram["a_w_s"][ia].rearrange("g t s -> t g s")),
                          wr=[wst_res], dma=asem)
                    sc.op("sp", I("dma_start", out=bsrc[:], in_=dram["a_b_in"][ia:ia + 1, AW:2 * AW]),
                          wr=[bsrc_res], dma=asem_b)
                    yield
                    for half in range(2):
                        bk, bres = next_bank()
                        fns = [I("transpose",
                            out=bk[:, j * 128:(j + 1) * 128], in_=wst[:, half * 4 + j, :], identity=ident[:]) for j in range(4)]
                        sc.op("pe", fns, rd=[wst_res, const_res], wr=[bres])
                        sc.op("dve", I("tensor_tensor",
                            out=wmf2[:, half * 4:half * 4 + 4, :], in0=bk[:].rearrange("p (a b) -> p a b", a=4),
                            in1=cmask[:].unsqueeze(1).broadcast_to([128, 4, 128]), op=ALU.mult),
                            rd=[bres, const_res], wr=[wmf2_res])
                    sc.op("act", I("activation", out=wmb[:], in_=wmf2[:], func=AF.Copy), rd=[wmf2_res], wr=[wmb_res])
                    sc.op("dve", I("memset", sel1[:], 0.0), wr=[sel1_res])
                    sc.op("dve", I("memset", sel1[0:1, :], 1.0), wr=[sel1_res])
                    sc.op("dve", I("memset", brow[:], 0.0), wr=[brow_res])
                    sc.op("dve", I("memset", rrow[:], 0.0), wr=[rrow_res])
                    sc.op("dve", I("memset", lrow[:], 0.0), wr=[lrow_res])
                    sc.op("dve", I("memset", lrow[32:33, :], 1.0), wr=[lrow_res])
                    sc.op("sp", I("dma_start", out=rrow[32:33, :], in_=dram["a_b_s"][ia:ia + 1, :]), wr=[rrow_res], dma=asem_r)
                    sc.op("sp", I("dma_start", out=lrow[0:1, :], in_=dram["a_vn_b"][ia:ia + 1, :]), wr=[lrow_res], dma=asem_l)
                    sc.op("dve", I("tensor_copy", out=brow[0:1, :], in_=bsrc[:]), rd=[bsrc_res], wr=[brow_res])
                    yield
                    for half in range(2):
                        bk, bres = next_bank()
                        sc.op("pe", I("matmul",
                            bk[0:1, :], lhsT=onesf[:, 0:1], rhs=wmf2[:, half * 4:half * 4 + 4, :], start=True, stop=True),
                            rd=[wmf2_res, const_res], wr=[bres])
                        sc.op("dve", I("tensor_copy", out=rrow[0:1, half * 512:(half + 1) * 512], in_=bk[0:1, :]),
                              rd=[bres], wr=[rrow_res])
                    yield
                    for q in range(ACJ // 4):
                        bk, bres = next_bank()
                        fns = []
                        for j in range(4):
                            cj = q * 4 + j
                            g = cj // 3
                            fns.append(I("matmul",
                                bk[:, j * 128:(j + 1) * 128], lhsT=lrow[:, cj * 128:(cj + 1) * 128], rhs=rrow[:, g * 128:(g + 1) * 128],
                                start=True, stop=True))
                        sc.op("pe", fns, rd=[lrow_res, rrow_res], wr=[bres])
                        evac_copy(Ssb[:, q * 4:q * 4 + 4, :], bk[:].rearrange("p (a b) -> p a b", a=4), rd=[bres], wr=[S_res])

                gen = prologue()
                if with_input:
                    def after_chunk(c):
                        if c in (1, 6, 10, 13):
                            next(gen, None)
                        if c == 3:
                            rn0()
                    input_phase(pro, after_chunk)
                    for _ in gen:
                        pass
                else:
                    if not rn0_done:
                        rn0()
                    for _ in gen:
                        pass
                sc.end_phase()
            rb = rb_persist
            hTs, hTs_res = hTp, hTp_res
            v = sb("a_v", (128, 4, AW), BF16, ph)
            v_res = [[Res(f"a_v{c}_{vp}") for vp in range(6)] for c in range(4)]
            uT = sb("a_uT", (128, ACJ, 512), BF16, ph)
            uT_res = [Res(f"a_uT{cj}") for cj in range(ACJ)]
            gt = [sb(f"a_gt{i}", (128, 512), F32, ph) for i in range(2)]
            gt_res = [Res(f"a_gt{i}") for i in range(2)]
            stats = sb("a_stats", (128, 4, 6, 6), F32, ph)
            stats_res = [[Res(f"a_st{c}_{vp}") for vp in range(6)] for c in range(4)]
            mv = sb("a_mv", (128, 4, 2), F32, ph)
            sdv = sb("a_sdv", (128, 4), F32, ph)
            nmr = sb("a_nmr", (128, 4), F32, ph)
            small_res = [Res(f"a_small{c}") for c in range(4)]

            def RN(tt):
                hT, hT_res = hTs[tt % 2], hTs_res[tt % 2]
                rmsnorm(rb, tt, gcol, lambda kc: hT[:, kc, :], lambda kc: hT_res[kc])

            def V(tt):
                hT, hT_res = hTs[tt % 2], hTs_res[tt % 2]
                for vp in range(6):
                    wv, wvr = load_w(w_in[ia, :, AW + vp * 512:AW + (vp + 1) * 512].rearrange("(kc p) f -> p kc f", p=128), (8, 512))
                    for c in range(4):
                        bk, bres = next_bank()
                        fns = [I("matmul",
                            bk[:], lhsT=hT[:, kc, c * 128:(c + 1) * 128], rhs=wv[:, kc, :], start=(kc == 0), stop=False)
                            for kc in range(KC)]
                        fns.append(I("matmul",
                            bk[:], lhsT=sel1[:], rhs=brow[:, vp * 512:(vp + 1) * 512], start=False, stop=True))
                        sc.op("pe", fns, rd=[wvr, brow_res, sel1_res] + hT_res, wr=[bres])
                        sc.op("act", I("activation",
                            out=v[:, c, vp * 512:(vp + 1) * 512], in_=bk[:], func=AF.Gelu_apprx_tanh),
                            rd=[bres], wr=[v_res[c][vp]])
                        sc.op("dve", I("bn_stats", out=stats[:, c, vp, :], in_=v[:, c, vp * 512:(vp + 1) * 512]),
                              rd=[v_res[c][vp]], wr=[stats_res[c][vp]])

            def ST(tt):
                for c in range(4):
                    sc.op("dve", I("bn_aggr", out=mv[:, c, :], in_=stats[:, c, :, :]),
                          rd=stats_res[c], wr=[small_res[c]])
                    sc.op("act", I("activation", out=sdv[:, c:c + 1], in_=mv[:, c, 1:2], func=AF.Sqrt,
                                   bias=epst[:, 0:1], scale=1.0),
                          rd=[const_res], wr=[small_res[c]])
                    sc.op("dve", I("reciprocal", out=sdv[:, c:c + 1], in_=sdv[:, c:c + 1]), wr=[small_res[c]])
                    sc.op("dve", I("scalar_tensor_tensor",
                        out=nmr[:, c:c + 1], in0=mv[:, c, 0:1], scalar=-1.0, in1=sdv[:, c:c + 1], op0=ALU.mult, op1=ALU.mult),
                        wr=[small_res[c]])
                    sc.op("act", I("activation", out=v[:, c, :], in_=v[:, c, :], func=AF.Identity,
                                   bias=nmr[:, c:c + 1], scale=sdv[:, c:c + 1]),
                          rd=[small_res[c]], wr=v_res[c])

            def U_group(tt, cj, wu, wur):
                hT, hT_res = hTs[tt % 2], hTs_res[tt % 2]
                cc = cj % 4
                bk, bres = next_bank()
                fns = [I("matmul",
                    bk[:], lhsT=wu[:, kc, cc * 128:(cc + 1) * 128], rhs=hT[:, kc, :], start=(kc == 0), stop=(kc == KC - 1))
                    for kc in range(KC)]
                sc.op("pe", fns, rd=[wur] + hT_res, wr=[bres])
                sc.op("act", I("activation",
                    out=uT[:, cj, :], in_=bk[:], func=AF.Gelu_apprx_tanh, bias=pv[:, bucol + cj:bucol + cj + 1], scale=1.0),
                    rd=[bres, pv_res], wr=[uT_res[cj]])

            def SP_group(cj):
                g = cj // 3
                bk, bres = next_bank()
                fns = [I("matmul",
                    bk[:, c * 128:(c + 1) * 128], lhsT=v[:, c, cj * 128:(cj + 1) * 128], rhs=wmb[:, g, :], start=True, stop=True)
                    for c in range(4)]
                sc.op("pe", fns, rd=[wmb_res] + [v_res[c][cj // 4] for c in range(4)], wr=[bres])
                gi = cj % 2
                sc.op("dve", I("scalar_tensor_tensor",
                    out=gt[gi][:].rearrange("p (a b) -> p a b", a=4), in0=bk[:].rearrange("p (a b) -> p a b", a=4),
                    scalar=pv[:, vgcol + cj:vgcol + cj + 1],
                    in1=Ssb[:, cj:cj + 1, :].broadcast_to([128, 4, 128]), op0=ALU.mult, op1=ALU.add),
                    rd=[bres, S_res, pv_res], wr=[gt_res[gi]])
                sc.op("dve", I("tensor_tensor", out=uT[:, cj, :], in0=gt[gi][:], in1=uT[:, cj, :], op=ALU.mult),
                      rd=[gt_res[gi]], wr=[uT_res[cj]])

            def USP(tt):
                lag = 8 if tt == 0 else 2
                wu = wur = None
                for cj in range(ACJ):
                    if cj % 4 == 0:
                        up = cj // 4
                        wu, wur = load_w(w_in[ia, :, up * 512:(up + 1) * 512].rearrange("(kc p) f -> p kc f", p=128), (8, 512))
                    U_group(tt, cj, wu, wur)
                    if cj >= lag:
                        SP_group(cj - lag)
                for cj in range(ACJ - lag, ACJ):
                    SP_group(cj)

            def WO(tt):
                for dcp in range(4):
                    sl = []
                    for h in range(2):
                        sl.append(load_w(w_out[ia, h * 1536:(h + 1) * 1536, dcp * 256:(dcp + 1) * 256].rearrange(
                            "(cj p) d -> p cj d", p=128), (12, 256)))
                    for dci in range(2):
                        dc = dcp * 2 + dci
                        bk, bres = next_bank()
                        fns = [I("matmul",
                            bk[:], lhsT=sl[cj // 12][0][:, cj % 12, dci * 128:(dci + 1) * 128], rhs=uT[:, cj, :],
                            start=(cj == 0), stop=(cj == ACJ - 1)) for cj in range(ACJ)]
                        sc.op("pe", fns, rd=[sl[0][1], sl[1][1]] + uT_res, wr=[bres])
                        add_to_x(dc, tt, bk, bres)

            if NT > 1:
                RN(1)
            V(0)
            ST(0)
            for tt in range(NT):
                USP(tt)
                if tt + 2 < NT:
                    RN(tt + 2)
                if tt + 1 < NT:
                    V(tt + 1)
                    ST(tt + 1)
                WO(tt)
            sc.end_phase()

    def mixer_b(l, rn0_done=False):
        gcol = PCOLS[f"norm_mix{l}"]
        bgcol = PCOLS["b_bgrp"]
        bscol = PCOLS["b_scale"]
        W = 528
        with ExitStack() as ph:
            rb = rb_persist
            hT, hT_res = hTp[0], hTp_res[0]
            pexts = [sb(f"b_pext{i}", (128, KC, W), F32, ph) for i in range(2)]
            pm_ress = [[Res(f"b_pm{i}_{kc}") for kc in range(KC)] for i in range(2)]
            ph_ress = [Res(f"b_phalo{i}") for i in range(2)]
            tA = sb("b_tA", (128, KC, W), F32, ph)
            tB = sb("b_tB", (128, 6, W), F32, ph)
            tA_res = Res("b_tA")
            tB_res = Res("b_tB")
            pooled = sb("b_pooled", (128, KC, 512), BF16, ph)
            pooled_res = [Res(f"b_pool{g}") for g in range(4)]
            y2 = pooled
            bsp = sb("b_bsp", (128, 8), F32, ph)
            bsp_res = Res("b_bsp")
            fx = sb("b_fx", (128, 2, 16), F32, ph)
            fx_res = Res("b_fx")
            sc.op("dve", I("tensor_tensor", out=bsp[:], in0=pv[:, bgcol:bgcol + 8], in1=pv[:, bscol:bscol + 8], op=ALU.mult),
                  rd=[pv_res], wr=[bsp_res])

            def RNb(tt):
                hT, hT_res = hTp[tt % 2], hTp_res[tt % 2]
                rmsnorm(rb, tt, gcol, lambda kc: hT[:, kc, :], lambda kc: hT_res[kc])

            def S1(tt):
                hT, hT_res = hTp[tt % 2], hTp_res[tt % 2]
                pext, pm_res, ph_res = pexts[tt % 2], pm_ress[tt % 2], ph_ress[tt % 2]
                if tt == 0:
                    sc.op("dve", I("memset", pext[:, :, 0:16], 0.0), wr=[ph_res])
                else:
                    sc.op("dve", I("tensor_copy", out=pext[:, :, 0:16], in_=pexts[(tt - 1) % 2][:, :, 512:528]),
                          rd=pm_ress[(tt - 1) % 2], wr=[ph_res])
                win = [load_w(dram["b_w_in"][0, :, h * 512:(h + 1) * 512].rearrange("(kc p) f -> p kc f", p=128), (8, 512))
                       for h in range(2)]
                for dc in range(KC):
                    bk, bres = next_bank()
                    wt, wtr = win[dc // 4]
                    fns = [I("matmul",
                        bk[:], lhsT=wt[:, kc, (dc % 4) * 128:(dc % 4 + 1) * 128], rhs=hT[:, kc, :], start=(kc == 0), stop=(kc == KC - 1))
                        for kc in range(KC)]
                    sc.op("pe", fns, rd=[wtr] + hT_res, wr=[bres])
                    sc.op("act", I("activation", out=pext[:, dc, 16:528], in_=bk[:], func=AF.Copy), rd=[bres], wr=[pm_res[dc]])

            def S2a(tt):
                pext, pm_res, ph_res = pexts[tt % 2], pm_ress[tt % 2], ph_ress[tt % 2]
                allp = pm_res + [ph_res]
                sc.op("dve", I("tensor_tensor", out=tA[:, :, 1:W], in0=pext[:, :, 1:W], in1=pext[:, :, 0:W - 1], op=ALU.add),
                      rd=allp, wr=[tA_res])
                sc.op("dve", I("tensor_tensor", out=tB[:, 0:6, 3:W], in0=tA[:, 2:8, 3:W], in1=tA[:, 2:8, 1:W - 2], op=ALU.add),
                      rd=[tA_res], wr=[tB_res])
                sc.op("dve", I("tensor_tensor", out=tA[:, 4:8, 7:W], in0=tB[:, 2:6, 7:W], in1=tB[:, 2:6, 3:W - 4], op=ALU.add),
                      rd=[tB_res], wr=[tA_res])
                sc.op("dve", I("tensor_tensor", out=tB[:, 4:6, 15:W], in0=tA[:, 6:8, 15:W], in1=tA[:, 6:8, 7:W - 8], op=ALU.add),
                      rd=[tA_res], wr=[tB_res])
                srcs = [tA[:, 0:2, :], tB[:, 0:2, :], tA[:, 4:6, :], tB[:, 4:6, :]]
                for g in range(4):
                    wn = B_WINDOWS[g]
                    src = srcs[g]
                    sc.op("dve", I("scalar_tensor_tensor",
                        out=pooled[:, 2 * g:2 * g + 2, :], in0=src[:, :, 16:W], scalar=1.0 / wn,
                        in1=pext[:, 2 * g:2 * g + 2, 16:W], op0=ALU.mult, op1=ALU.subtract),
                        rd=[tA_res, tB_res] + allp, wr=[pooled_res[g]])
                    if tt == 0:
                        n = wn - 1
                        sc.op("dve", I("tensor_tensor", out=fx[:, :, 0:n], in0=src[:, :, 16:16 + n],
                                       in1=invc[:, 0:n].unsqueeze(1).broadcast_to([128, 2, n]), op=ALU.mult),
                              rd=[tA_res, tB_res, invc_res], wr=[fx_res])
                        sc.op("dve", I("tensor_tensor", out=pooled[:, 2 * g:2 * g + 2, 0:n], in0=fx[:, :, 0:n],
                                       in1=pext[:, 2 * g:2 * g + 2, 16:16 + n], op=ALU.subtract),
                              rd=[fx_res] + allp, wr=[pooled_res[g]])

            def S2b(tt):
                wg, wgr = load_w(dram["b_w_grp"][0].rearrange("g (kk p) e -> p (g kk) e", p=128), (8, 256))
                for g in range(4):
                    grp_banks = []
                    for ec in range(2):
                        bk, bres = next_bank()
                        fns = [I("matmul",
                            bk[:], lhsT=wg[:, g * 2 + kk, ec * 128:(ec + 1) * 128], rhs=pooled[:, 2 * g + kk, :],
                            start=(kk == 0), stop=(kk == 1)) for kk in range(2)]
                        sc.op("pe", fns, rd=[wgr, pooled_res[g]], wr=[bres])
                        grp_banks.append((bk, bres))
                    for ec in range(2):
                        oc = 2 * g + ec
                        bk, bres = grp_banks[ec]
                        sc.op("act", I("activation", out=y2[:, oc, :], in_=bk[:], func=AF.Identity,
                                       bias=bsp[:, oc:oc + 1], scale=pv[:, bscol + oc:bscol + oc + 1]),
                              rd=[bres, pv_res, bsp_res], wr=[pooled_res[g]])
                wo = [load_w(dram["b_w_out"][0, :, h * 512:(h + 1) * 512].rearrange("(kc p) f -> p kc f", p=128), (8, 512))
                      for h in range(2)]
                for dc in range(KC):
                    bk, bres = next_bank()
                    wt, wtr = wo[dc // 4]
                    fns = [I("matmul",
                        bk[:], lhsT=wt[:, kc, (dc % 4) * 128:(dc % 4 + 1) * 128], rhs=y2[:, kc, :], start=(kc == 0), stop=(kc == KC - 1))
                        for kc in range(KC)]
                    sc.op("pe", fns, rd=[wtr] + pooled_res, wr=[bres])
                    add_to_x(dc, tt, bk, bres)

            if not rn0_done:
                RNb(0)
            S1(0)
            if NT > 1:
                RNb(1)
            for tt in range(NT):
                S2a(tt)
                if tt + 1 < NT:
                    S1(tt + 1)
                if tt + 2 < NT:
                    RNb(tt + 2)
                S2b(tt)
            sc.end_phase()

    def mixer_c(l, rn0_done=False):
        gcol = PCOLS[f"norm_mix{l}"]
        wcol = PCOLS["c_wdw"]
        bdcol = PCOLS["c_bdw"]
        lgcol = PCOLS["c_lng"]
        lbcol = PCOLS["c_lnb"]
        W = 544
        with ExitStack() as ph:
            rb = rb_persist
            hT, hT_res = hTp[0], hTp_res[0]
            identb = sb("c_identb", (128, 128), BF16, ph)
            identb_res = Res("c_identb")
            yext = sb("c_yext", (128, KC, W), BF16, ph)
            ym_res = [Res(f"c_ym{kc}") for kc in range(KC)]
            yh_res = Res("c_yhalo")
            dg = [sb(f"c_dg{i}", (128, CK, 128), BF16, ph) for i in range(2)]
            dg_res = [Res(f"c_dg{i}") for i in range(2)]
            accs = [sb(f"c_acc{i}", (128, KC, 512), F32, ph) for i in range(2)]
            accs_res = [[Res(f"c_acc{i}_{kc}") for kc in range(KC)] for i in range(2)]
            msq = sb("c_msq", (128, 512), F32, ph)
            msq_res = Res("c_msq")
            sg = [sb(f"c_sg{i}", (128, 512), F32, ph) for i in range(2)]
            sg_res = [Res(f"c_sg{i}") for i in range(2)]
            mean = sb("c_mean", (128, 512), F32, ph)
            rstd = sb("c_rstd", (128, 512), F32, ph)
            mean_res = Res("c_mean")
            rstd_res = Res("c_rstd")
            nr = [sb(f"c_nr{i}", (128, 512), F32, ph) for i in range(2)]
            nr_res = [Res(f"c_nr{i}") for i in range(2)]
            z, z_res = hTp[1], hTp_res[1]
            sc.op("dve", I("tensor_copy", out=identb[:], in_=ident[:]), rd=[const_res], wr=[identb_res])

            def RNc(tt):
                rmsnorm(rb, tt, gcol, lambda kc: hT[:, kc, :], lambda kc: hT_res[kc])

            def frontA(tt):
                if tt == 0:
                    sc.op("dve", I("memset", yext[:, :, 0:32], 0.0), wr=[yh_res])
                else:
                    sc.op("dve", I("tensor_copy", out=yext[:, :, 0:32], in_=yext[:, :, 512:544]), rd=ym_res, wr=[yh_res])
                wa = [load_w(dram["c_w_in"][0, :, h * 512:(h + 1) * 512].rearrange("(kc p) f -> p kc f", p=128), (8, 512))
                      for h in range(2)]
                wgt = [load_w(dram["c_w_in"][0, :, D + h * 512:D + (h + 1) * 512].rearrange("(kc p) f -> p kc f", p=128), (8, 512))
                       for h in range(2)]
                for dc in range(KC):
                    bka, bresa = next_bank()
                    bkg, bresg = next_bank()
                    for (bk, bres, wsl) in ((bka, bresa, wa), (bkg, bresg, wgt)):
                        wt, wtr = wsl[dc // 4]
                        fns = [I("matmul",
                            bk[:], lhsT=wt[:, kc, (dc % 4) * 128:(dc % 4 + 1) * 128], rhs=hT[:, kc, :], start=(kc == 0), stop=(kc == KC - 1))
                            for kc in range(KC)]
                        sc.op("pe", fns, rd=[wtr] + hT_res, wr=[bres])
                    si = dc % 2
                    sc.op("act", I("activation", out=sg[si][:], in_=bkg[:], func=AF.Sigmoid),
                          rd=[bresg], wr=[sg_res[si]])
                    sc.op("dve", I("tensor_tensor", out=yext[:, dc, 32:W], in0=bka[:], in1=sg[si][:], op=ALU.mult),
                          rd=[bresa, sg_res[si]], wr=[ym_res[dc]])

            def build(dc):
                di = dc % 2
                col = wcol + dc * CK
                sc.op("dve", I("tensor_tensor", out=dg[di][:],
                               in0=identb[:].unsqueeze(1).broadcast_to([128, CK, 128]),
                               in1=pv[:, col:col + CK].unsqueeze(2).broadcast_to([128, CK, 128]), op=ALU.mult),
                      rd=[identb_res, pv_res], wr=[dg_res[di]])

            def prebuild(tt):
                build(0)
                build(1)

            def frontB(tt, ln_tt=None):
                ac, ac_res = accs[tt % 2], accs_res[tt % 2]
                for dc in range(KC):
                    di = dc % 2
                    bk, bres = next_bank()
                    fns = [I("matmul", bk[:], lhsT=dg[di][:, k, :], rhs=yext[:, dc, 2 + k:2 + k + 512], start=(k == 0), stop=(k == CK - 1))
                           for k in range(CK)]
                    sc.op("pe", fns, rd=[dg_res[di], ym_res[dc], yh_res], wr=[bres])
                    sc.op("act", I("activation", out=ac[:, dc, :], in_=bk[:], func=AF.Identity,
                                   bias=pv[:, bdcol + dc:bdcol + dc + 1], scale=1.0),
                          rd=[bres, pv_res], wr=[ac_res[dc]])
                    if dc + 2 < KC:
                        build(dc + 2)
                    if ln_tt is not None:
                        if dc == 0:
                            LN_head(ln_tt)
                        LN_chunk(ln_tt, dc)
                bkm, bresm = next_bank()
                bkq, bresq = next_bank()
                for dc in range(KC):
                    y0, y0r = next_sq()
                    sc.op("act", I("activation", out=y0[:], in_=ac[:, dc, :], func=AF.Copy),
                          rd=[ac_res[dc]], wr=[y0r])
                    sc.op("pe", I("matmul", bkm[:], lhsT=onesm[:], rhs=y0[:], start=(dc == 0), stop=(dc == KC - 1)),
                          rd=[y0r, const_res], wr=[bresm] if dc == 0 else [])
                    y1, y1r = next_sq()
                    sc.op("act", I("activation", out=y1[:], in_=ac[:, dc, :], func=AF.Square),
                          rd=[ac_res[dc]], wr=[y1r])
                    sc.op("pe", I("matmul", bkq[:], lhsT=onesm[:], rhs=y1[:], start=(dc == 0), stop=(dc == KC - 1)),
                          rd=[y1r, const_res], wr=[bresq] if dc == 0 else [])
                bresm.w = {"E_pe": sc.cnt["E_pe"]}
                bresq.w = {"E_pe": sc.cnt["E_pe"]}
                sc.op("act", I("activation", out=mean[:], in_=bkm[:], func=AF.Copy), rd=[bresm], wr=[mean_res])
                sc.op("dve", I("tensor_copy", out=rstd[:], in_=bkq[:]), rd=[bresq], wr=[rstd_res])

            def LN_head(tt):
                sc.op("dve", I("tensor_tensor", out=msq[:], in0=mean[:], in1=mean[:], op=ALU.mult), rd=[mean_res], wr=[msq_res])
                sc.op("dve", I("tensor_tensor", out=rstd[:], in0=rstd[:], in1=msq[:], op=ALU.subtract), rd=[msq_res], wr=[rstd_res])
                sc.op("act", I("activation", out=rstd[:], in_=rstd[:], func=AF.Sqrt, bias=epst[:, 0:1], scale=1.0),
                      rd=[const_res], wr=[rstd_res])
                sc.op("dve", I("reciprocal", out=rstd[:], in_=rstd[:]), wr=[rstd_res])

            def zbuf(tt):
                if NT > 1 and tt == NT - 1:
                    return hTp[0], hTp_res[0]
                return hTp[1], hTp_res[1]

            def LN_chunk(tt, dc):
                z, z_res = zbuf(tt)
                ac, ac_res = accs[tt % 2], accs_res[tt % 2]
                ni = dc % 2
                sc.op("dve", I("tensor_tensor", out=nr[ni][:], in0=ac[:, dc, :], in1=mean[:], op=ALU.subtract),
                      rd=[ac_res[dc], mean_res], wr=[nr_res[ni]])
                sc.op("dve", I("tensor_tensor", out=nr[ni][:], in0=nr[ni][:], in1=rstd[:], op=ALU.mult),
                      rd=[rstd_res], wr=[nr_res[ni]])
                sc.op("act", I("activation",
                    out=z[:, dc, :], in_=nr[ni][:], func=AF.Silu, bias=pv[:, lbcol + dc:lbcol + dc + 1],
                    scale=pv[:, lgcol + dc:lgcol + dc + 1]), rd=[nr_res[ni], pv_res], wr=[z_res[dc]])

            def backPE(tt):
                z, z_res = zbuf(tt)
                wo = [load_w(dram["c_w_out"][0, :, h * 512:(h + 1) * 512].rearrange("(kc p) f -> p kc f", p=128), (8, 512))
                      for h in range(2)]
                for dc in range(KC):
                    bk, bres = next_bank()
                    wt, wtr = wo[dc // 4]
                    fns = [I("matmul",
                        bk[:], lhsT=wt[:, kc, (dc % 4) * 128:(dc % 4 + 1) * 128], rhs=z[:, kc, :], start=(kc == 0), stop=(kc == KC - 1))
                        for kc in range(KC)]
                    sc.op("pe", fns, rd=[wtr] + z_res, wr=[bres])
                    add_to_x(dc, tt, bk, bres)

            if not rn0_done:
                RNc(0)
            prebuild(0)
            frontA(0)
            if NT > 1:
                RNc(1)
            frontB(0)
            for tt in range(NT):
                if tt + 1 < NT:
                    prebuild(tt + 1)
                    frontA(tt + 1)
                    if tt + 2 < NT:
                        RNc(tt + 2)
                    frontB(tt + 1, ln_tt=tt)
                    if tt + 2 == NT:
                        LN_head(tt + 1)
                        for dc in range(KC):
                            LN_chunk(tt + 1, dc)
                elif NT == 1:
                    LN_head(tt)
                    for dc in range(KC):
                        LN_chunk(tt, dc)
                backPE(tt)
            sc.end_phase()

    NOB = 2
    osem = [sc.dma_sem(f"xout{i}") for i in range(NOB)]

    def out_setup(stack):
        ost = [sb(f"ost{i}", (128, D), F32, stack) for i in range(NOB)]
        ost_res = [Res(f"ost{i}") for i in range(NOB)]
        if final:
            gbc = sb("f_gbc", (128, D), F32, stack)
            gbc_res = Res("f_gbc")
            gsem = sc.dma_sem("f_gbc")
            junk = [sb(f"f_junk{i}", (128, 512), F32, stack) for i in range(2)]
            junk_res = [Res(f"f_junk{i}") for i in range(2)]
            ss = sb("f_ss", (128, NOB, 4), F32, stack)
            ss_res = [Res(f"f_ss{i}") for i in range(NOB)]
            sc.op("sp", I("dma_start", out=gbc[:], in_=dram["norm_final"].partition_broadcast(128)), wr=[gbc_res], dma=gsem)

        def tail(tt):
            for ci in range(4):
                c = tt * 4 + ci
                ob = c % NOB
                bks = []
                for half in range(2):
                    bk, bres = next_bank()
                    fns = []
                    rds = [const_res]
                    for j in range(4):
                        dc = half * 4 + j
                        rds.append(xT_res[dc][tt])
                        fns.append(I("transpose", out=bk[:, j * 128:(j + 1) * 128], in_=xT[:, dc, c * 128:(c + 1) * 128],
                                     identity=ident[:]))
                    sc.op("pe", fns, rd=rds, wr=[bres])
                    bks.append((bk, bres))
                if final:
                    for half in range(2):
                        bk, bres = bks[half]
                        sc.op("act", I("activation", out=junk[half][:], in_=bk[:], func=AF.Square,
                                       accum_out=ss[:, ob, half:half + 1]),
                              rd=[bres], wr=[junk_res[half], ss_res[ob]])
                    sc.op("dve", I("tensor_tensor", out=ss[:, ob, 2:3], in0=ss[:, ob, 0:1], in1=ss[:, ob, 1:2], op=ALU.add),
                          wr=[ss_res[ob]])
                    sc.op("act", I("activation", out=ss[:, ob, 3:4], in_=ss[:, ob, 2:3], func=AF.Sqrt,
                                   bias=epst[:, 0:1], scale=1.0 / D), rd=[const_res], wr=[ss_res[ob]])
                    sc.op("dve", I("reciprocal", out=ss[:, ob, 3:4], in_=ss[:, ob, 3:4]), wr=[ss_res[ob]])
                    for half in range(2):
                        bk, bres = bks[half]
                        sc.op("dve", I("scalar_tensor_tensor", out=ost[ob][:, half * 512:(half + 1) * 512], in0=bk[:],
                                       scalar=ss[:, ob, 3:4], in1=gbc[:, half * 512:(half + 1) * 512],
                                       op0=ALU.mult, op1=ALU.mult),
                              rd=[bres, ss_res[ob], gbc_res], wr=[ost_res[ob]])
                else:
                    for half in range(2):
                        bk, bres = bks[half]
                        evac_copy(ost[ob][:, half * 512:(half + 1) * 512], bk[:], rd=[bres], wr=[ost_res[ob]])
                sc.op("sp", I("dma_start", out=out_d[c * 128:(c + 1) * 128, :], in_=ost[ob][:]),
                      rd=[ost_res[ob]], dma=osem[ob])
        return tail

    first_is_a = bool(layers) and layers[0] % 3 == 0
    if not first_is_a:
        with ExitStack() as ph:
            input_phase(ph)
            sc.end_phase()
    def make_rn0(lnext):
        g = PCOLS[f"norm_mix{lnext}"]
        return lambda: rmsnorm(rb_persist, 0, g, lambda kc: hTp[0][:, kc, :], lambda kc: hTp_res[0][kc])

    for li, l in enumerate(layers):
        kind = l % 3
        pre = li > 0
        if kind == 0:
            mixer_a(l, l // 3, with_input=(li == 0), rn0_done=pre)
        elif kind == 1:
            mixer_b(l, rn0_done=pre)
        else:
            mixer_c(l, rn0_done=pre)
        last = li == len(layers) - 1
        mlp(l, tail_setup=out_setup if last else None, next_rn0=None if last else make_rn0(layers[li + 1]))
    if not layers:
        with ExitStack() as ph:
            tail = out_setup(ph)
            for tt in range(NT):
                tail(tt)
    sc.wait_all("sp", osem)
    sc.wait_all("sp", ["E_pe", "E_act", "E_dve"] + ring_sem)

    with nc.Block() as block:
        sc.emit(block)
    es.close()
    return nc


S_FULL = 2048
N_CORES = 8
_WEIGHT_KEYS = {
    0: ("a_w_in", "a_b_in", "a_vn_b", "a_w_s", "a_b_s", "a_w_out"),
    1: ("b_w_in", "b_w_grp", "b_w_out"),
    2: ("c_w_in", "c_w_out"),
}


def _in_maps(inp, x_shards, pvec, layers, final=True):
    kinds = sorted(set(l % 3 for l in layers))
    shared = {"pvec": pvec}
    if final:
        shared["norm_final"] = np.ascontiguousarray(np.asarray(inp["norm_final"], dtype=np.float32))
    for k in kinds:
        for name in _WEIGHT_KEYS[k]:
            a = np.ascontiguousarray(np.asarray(inp[name], dtype=np.float32))
            if name == "a_b_s":
                a = a.reshape(2, 8 * 128)
            shared[name] = a
    if layers:
        shared["m_w1"] = np.ascontiguousarray(np.asarray(inp["m_w1"], dtype=np.float32))
        shared["m_w2"] = np.ascontiguousarray(np.asarray(inp["m_w2"], dtype=np.float32))
    maps = []
    for xs in x_shards:
        m = dict(shared)
        m["x"] = np.ascontiguousarray(xs)
        maps.append(m)
    return maps


LAUNCH_PLAN = [([0, 1, 2, 3], True)]


def kernel(**inputs):
    x = np.asarray(inputs["x"], dtype=np.float32)
    pvec = build_pvec(inputs)
    cur = [x[i] for i in range(N_CORES)]
    for layers, final in LAUNCH_PLAN:
        nc = build_program(S_FULL, layers, final)
        maps = _in_maps(inputs, cur, pvec, layers, final)
        res = run_bass_kernel_spmd(nc, maps, core_ids=list(range(N_CORES)))
        cur = [np.asarray(r["out"], dtype=np.float32) for r in res.results]
    return np.stack(cur, axis=0)
```

```python
import numpy as np
from contextlib import ExitStack

import concourse.bass as bass
import concourse.mybir as mybir
from concourse.bass_utils import run_bass_kernel_spmd

F32 = mybir.dt.float32
BF16 = mybir.dt.bfloat16
AF = mybir.ActivationFunctionType
ALU = mybir.AluOpType

D = 1024
KC = 8
AW = 3072
ACJ = 24
HID = 4096
EPS = 1e-6
B_WINDOWS = (2, 4, 8, 16)
CK = 31
NSLOT = 5
NSQ = 6
SLOT_ELEMS = 4096


def _pvec_layout():
    cols = {}
    n = 0

    def add(name, ncols):
        nonlocal n
        cols[name] = n
        n += ncols

    for l in range(4):
        add(f"norm_mix{l}", 8)
        add(f"norm_mlp{l}", 8)
    add("norm_final", 8)
    for ia in range(2):
        add(f"a_bu{ia}", ACJ)
        add(f"a_vng{ia}", ACJ)
    add("b_bgrp", 8)
    add("b_scale", 8)
    add("c_wdw", CK * 8)
    add("c_bdw", 8)
    add("c_lng", 8)
    add("c_lnb", 8)
    return cols, n


PCOLS, NPCOL = _pvec_layout()


def _cols(v):
    v = np.asarray(v, dtype=np.float32).reshape(-1, 128)
    return np.ascontiguousarray(v.T)


def build_pvec(inp):
    pv = np.zeros((128, NPCOL), dtype=np.float32)

    def put(name, v):
        c = _cols(v)
        pv[:, PCOLS[name]:PCOLS[name] + c.shape[1]] = c

    for l in range(4):
        put(f"norm_mix{l}", inp["norm_mix"][l])
        put(f"norm_mlp{l}", inp["norm_mlp"][l])
    put("norm_final", inp["norm_final"])
    for ia in range(2):
        put(f"a_bu{ia}", inp["a_b_in"][ia][:AW])
        put(f"a_vng{ia}", inp["a_vn_g"][ia])
    put("b_bgrp", np.asarray(inp["b_b_grp"][0]).reshape(-1))
    put("b_scale", inp["b_scale"][0])
    wdw = np.asarray(inp["c_w_dw"][0], dtype=np.float32).reshape(CK, 8, 128)
    pv[:, PCOLS["c_wdw"]:PCOLS["c_wdw"] + CK * 8] = wdw.transpose(2, 1, 0).reshape(128, 8 * CK)
    put("c_bdw", inp["c_b_dw"][0])
    put("c_lng", inp["c_ln_g"][0])
    put("c_lnb", inp["c_ln_b"][0])
    return pv


def I(_method, *args, **kwargs):
    return lambda e: getattr(e, _method)(*args, **kwargs)


class Res:
    __slots__ = ("name", "w", "r")
    init_w = {}

    def __init__(self, name):
        self.name = name
        self.w = dict(Res.init_w)
        self.r = {}


class Sched:
    ENGS = ("pe", "act", "dve", "pool", "sp")

    def __init__(self, nc, es):
        self.nc = nc
        self.es = es
        self.q = {e: [] for e in self.ENGS}
        self.sems = {}
        self.cnt = {}
        self.waited = {}
        for e in self.ENGS:
            self._mksem("E_" + e)

    def _mksem(self, key):
        h = self.es.enter_context(self.nc.semaphore(key))
        self.sems[key] = h
        self.cnt[key] = 0
        return key

    def dma_sem(self, name):
        return self._mksem("D_" + name)

    def _wait(self, eng, key, val):
        if eng == "pe" and key == "E_pe":
            return
        if self.waited.get((eng, key), 0) >= val:
            return
        self.waited[(eng, key)] = val
        self.q[eng].append(("w", key, val))

    def op(self, eng, fns, rd=(), wr=(), dma=None):
        if not isinstance(fns, (list, tuple)):
            fns = [fns]
        deps = {}
        for r in rd:
            for k, v in r.w.items():
                if deps.get(k, 0) < v:
                    deps[k] = v
        for r in wr:
            for k, v in r.w.items():
                if deps.get(k, 0) < v:
                    deps[k] = v
            for k, v in r.r.items():
                if deps.get(k, 0) < v:
                    deps[k] = v
        for k, v in deps.items():
            self._wait(eng, k, v)
        if dma is None:
            key, inc = "E_" + eng, 1
        else:
            key, inc = dma, 16
        self.cnt[key] += inc
        val = self.cnt[key]
        for f in fns[:-1]:
            self.q[eng].append(("i", f, None, 0))
        self.q[eng].append(("i", fns[-1], key, inc))
        for r in rd:
            if r.r.get(key, 0) < val:
                r.r[key] = val
        for r in wr:
            r.w = {key: val}
            r.r = {}
        return (key, val)

    def barrier(self, engs=("pe", "act", "dve", "sp")):
        for e in engs:
            for f in engs:
                if e != f and self.cnt["E_" + f] > 0:
                    self._wait(e, "E_" + f, self.cnt["E_" + f])

    def end_phase(self):
        Res.init_w = {"E_" + e: self.cnt["E_" + e] for e in ("pe", "act", "dve", "pool") if self.cnt["E_" + e] > 0}

    def wait_all(self, eng, keys):
        for k in keys:
            if self.cnt[k] > 0:
                self._wait(eng, k, self.cnt[k])

    def emit(self, block):
        def play(eng_name):
            def body(e):
                for it in self.q[eng_name]:
                    if it[0] == "w":
                        e.wait_ge(self.sems[it[1]], it[2])
                    else:
                        ins = it[1](e)
                        if it[2] is not None:
                            ins.then_inc(self.sems[it[2]], it[3])
            return body
        if self.q["sp"]:
            block.sync(play("sp"))
        if self.q["pool"]:
            block.gpsimd(play("pool"))
        if self.q["act"]:
            block.scalar(play("act"))
        if self.q["dve"]:
            block.vector(play("dve"))
        if self.q["pe"]:
            block.tensor(play("pe"))


def build_program(S, layers, final, used_inputs=None):
    assert S % 512 == 0
    NT = S // 512
    NCH = S // 128
    nc = bass.Bass("TRN2", target_bir_lowering=False)
    es = ExitStack()

    kinds = sorted(set(l % 3 for l in layers))
    dram = {}

    def din(name, shape):
        dram[name] = nc.dram_tensor(name, list(shape), F32, kind="ExternalInput").ap()

    din("x", (S, D))
    din("pvec", (128, NPCOL))
    if final:
        din("norm_final", (D,))
    if 0 in kinds:
        din("a_w_in", (2, D, 2 * AW))
        din("a_b_in", (2, 2 * AW))
        din("a_vn_b", (2, AW))
        din("a_w_s", (2, 8, 128, 128))
        din("a_b_s", (2, 8 * 128))
        din("a_w_out", (2, AW, D))
    if 1 in kinds:
        din("b_w_in", (1, D, D))
        din("b_w_grp", (1, 4, 256, 256))
        din("b_w_out", (1, D, D))
    if 2 in kinds:
        din("c_w_in", (1, D, 2 * D))
        din("c_w_out", (1, D, D))
    if layers:
        din("m_w1", (4, D, HID))
        din("m_w2", (4, HID, D))
    out_d = nc.dram_tensor("out", [S, D], F32, kind="ExternalOutput").ap()

    Res.init_w = {}
    sc = Sched(nc, es)

    uid = [0]

    def sb(name, shape, dt, stack=None):
        uid[0] += 1
        return (stack or es).enter_context(nc.sbuf_tensor(f"{name}_{uid[0]}", list(shape), dt))

    xT = sb("xT", (128, KC, S), F32)
    xT_res = [[Res(f"xT{kc}_{tt}") for tt in range(NT)] for kc in range(KC)]
    pv = sb("pv", (128, NPCOL), F32)
    pv_res = Res("pv")
    ident = sb("ident", (128, 128), F32)
    onesf = sb("onesf", (128, 128), F32)
    cmask = sb("cmask", (128, 128), F32)
    invc = sb("invc", (128, 16), F32)
    onesm = sb("onesm", (128, 128), BF16)
    epst = sb("epst", (128, 1), F32)
    const_res = Res("const")
    ring = [sb(f"ring{i}", (128, SLOT_ELEMS), BF16) for i in range(NSLOT)]
    ring_res = [Res(f"ring{i}") for i in range(NSLOT)]
    ring_sem = [sc.dma_sem(f"ring{i}") for i in range(NSLOT)]
    ring_next = [0]

    rb_persist = {
        "sq": [sb(f"rsq{i}", (128, 512), BF16) for i in range(NSQ)],
        "sq_res": [Res(f"rsq{i}") for i in range(NSQ)],
        "sd": [sb(f"rsd{i}", (128, 512), F32) for i in range(2)],
        "sd_res": [Res(f"rsd{i}") for i in range(2)],
        "n": 0,
        "k": 0,
    }

    def next_sq():
        i = rb_persist["k"] % NSQ
        rb_persist["k"] += 1
        return rb_persist["sq"][i], rb_persist["sq_res"][i]

    hTp = [sb(f"hTp{i}", (128, KC, 512), BF16) for i in range(2)]
    hTp_res = [[Res(f"hTp{i}_{kc}") for kc in range(KC)] for i in range(2)]

    banks = [es.enter_context(nc.psum_tensor(f"bank{i}", [128, 512], F32)) for i in range(8)]
    bank_res = [Res(f"bank{i}") for i in range(8)]
    bank_next = [0]

    def next_bank():
        i = bank_next[0]
        bank_next[0] = (i + 1) % 8
        return banks[i], bank_res[i]

    def load_w(src_ap, view):
        i = ring_next[0]
        ring_next[0] = (i + 1) % NSLOT
        a, b = view
        assert a * b <= SLOT_ELEMS
        dst = ring[i][:, 0:a * b].rearrange("p (a b) -> p a b", a=a)
        sc.op("pool", I("dma_start", out=dst, in_=src_ap, max_dma_last_dim=8192),
              wr=[ring_res[i]], dma=ring_sem[i])
        return dst, ring_res[i]

    psem = sc.dma_sem("params")
    sc.op("sp", I("dma_start", out=pv[:], in_=dram["pvec"]), wr=[pv_res], dma=psem)
    sc.op("pool", I("memset", onesf[:], 1.0), wr=[const_res])
    sc.op("pool", I("memset", onesm[:], 1.0 / D), wr=[const_res])
    sc.op("pool", I("memset", epst[:], EPS), wr=[const_res])
    sc.op("pool", I("affine_select", out=ident[:], in_=onesf[:], pattern=[[1, 128]],
                                            compare_op=ALU.is_equal, fill=0.0, base=0, channel_multiplier=-1),
          rd=[const_res], wr=[const_res])
    sc.op("pool", I("affine_select", out=cmask[:], in_=onesf[:], pattern=[[1, 128]],
                    compare_op=ALU.is_ge, fill=0.0, base=0, channel_multiplier=-1),
          rd=[const_res], wr=[const_res])
    invc_res = Res("invc")
    for j in range(16):
        sc.op("pool", I("memset", invc[:, j:j + 1], 1.0 / (j + 1)), wr=[invc_res])

    evac_rr = [0]

    def evac_copy(out_ap, in_ap, rd, wr):
        evac_rr[0] ^= 1
        if evac_rr[0]:
            sc.op("act", I("activation", out=out_ap, in_=in_ap, func=AF.Copy), rd=rd, wr=wr)
        else:
            sc.op("dve", I("tensor_copy", out=out_ap, in_=in_ap), rd=rd, wr=wr)

    def input_phase(stack, after_chunk=None):
        NXB = 4
        xin = [sb(f"xin{i}", (128, D), F32, stack) for i in range(NXB)]
        xin_res = [Res(f"xin{i}") for i in range(NXB)]
        xin_sem = [sc.dma_sem(f"xin{i}") for i in range(NXB)]
        for c in range(NCH):
            b = c % NXB
            tt = c // 4
            sc.op("sp", I("dma_start", out=xin[b][:], in_=dram["x"][c * 128:(c + 1) * 128, :]),
                  wr=[xin_res[b]], dma=xin_sem[b])
            for half in range(2):
                bk, bres = next_bank()
                fns = []
                for j in range(4):
                    dc = half * 4 + j
                    fns.append(I("transpose",
                        out=bk[:, j * 128:(j + 1) * 128], in_=xin[b][:, dc * 128:(dc + 1) * 128], identity=ident[:]))
                sc.op("pe", fns, rd=[xin_res[b], const_res], wr=[bres])
                evac_copy(xT[:, half * 4:half * 4 + 4, c * 128:(c + 1) * 128],
                          bk[:].rearrange("p (a b) -> p a b", a=4),
                          rd=[bres], wr=[xT_res[half * 4 + j][tt] for j in range(4)])
            if after_chunk is not None:
                after_chunk(c)

    def make_rms_bufs(stack):
        d = {
            "sq": [sb(f"rsq{i}", (128, 512), BF16, stack) for i in range(2)],
            "sq_res": [Res(f"rsq{i}") for i in range(2)],
            "sd": [sb(f"rsd{i}", (128, 512), F32, stack) for i in range(2)],
            "sd_res": [Res(f"rsd{i}") for i in range(2)],
            "n": 0,
        }
        return d

    def rms_rstd(rb, tt, dst=None, dst_res=None):
        t0 = tt * 512
        par = rb["n"] % 2
        rb["n"] += 1
        bk, bres = next_bank()
        for kc in range(KC):
            sq, sqr = next_sq()
            sc.op("act", I("activation", out=sq[:], in_=xT[:, kc, t0:t0 + 512], func=AF.Square),
                  rd=[xT_res[kc][tt]], wr=[sqr])
            sc.op("pe", I("matmul", bk[:], lhsT=onesm[:], rhs=sq[:], start=(kc == 0), stop=(kc == KC - 1)),
                  rd=[sqr, const_res], wr=[bres] if kc == 0 else [], )
        if dst is None:
            sd, sdr = rb["sd"][par][:], rb["sd_res"][par]
        else:
            sd, sdr = dst, dst_res
        bres.w = {"E_pe": sc.cnt["E_pe"]}
        sc.op("act", I("activation", out=sd, in_=bk[:], func=AF.Sqrt, bias=epst[:, 0:1], scale=1.0),
              rd=[bres, const_res], wr=[sdr])
        sc.op("dve", I("reciprocal", out=sd, in_=sd), rd=[], wr=[sdr])
        return sd, sdr

    def rmsnorm(rb, tt, gcol, out_fn, out_res_fn):
        t0 = tt * 512
        sd, sdr = rms_rstd(rb, tt)
        for kc in range(KC):
            o = out_fn(kc)
            sc.op("dve", I("scalar_tensor_tensor",
                out=o, in0=xT[:, kc, t0:t0 + 512], scalar=pv[:, gcol + kc:gcol + kc + 1], in1=sd,
                op0=ALU.mult, op1=ALU.mult),
                rd=[xT_res[kc][tt], sdr, pv_res], wr=[out_res_fn(kc)])

    def add_to_x(dc, tt, bk, bres):
        t0 = tt * 512
        sc.op("dve", I("tensor_tensor", out=xT[:, dc, t0:t0 + 512], in0=bk[:], in1=xT[:, dc, t0:t0 + 512], op=ALU.add),
              rd=[bres], wr=[xT_res[dc][tt]])

    def mlp(l, tail_setup=None, next_rn0=None):
        with ExitStack() as ph:
            rb = rb_persist
            tail = tail_setup(ph) if tail_setup is not None else None
            hT = sb("m_hT", (128, KC, S), BF16, ph)
            hT_res = [[Res(f"m_hT{kc}_{tt}") for tt in range(NT)] for kc in range(KC)]
            aT = [sb(f"m_aT{i}", (128, 4, 512), BF16, ph) for i in range(2)]
            aT_res = [[Res(f"m_aT{i}_{fc}") for fc in range(4)] for i in range(2)]
            rl = [sb(f"m_rl{i}", (128, 512), F32, ph) for i in range(2)]
            rl_res = [Res(f"m_rl{i}") for i in range(2)]
            gcol = PCOLS[f"norm_mlp{l}"]
            rstd_b = sb("m_rstd", (128, NT, 512), F32, ph)
            rstd_res = [Res(f"m_rstd{tt}") for tt in range(NT)]
            def prep(tt):
                for kc in range(KC):
                    sc.op("dve", I("tensor_scalar", out=hT[:, kc, tt * 512:(tt + 1) * 512], in0=xT[:, kc, tt * 512:(tt + 1) * 512],
                                   scalar1=pv[:, gcol + kc:gcol + kc + 1], scalar2=None, op0=ALU.mult),
                          rd=[xT_res[kc][tt], pv_res], wr=[hT_res[kc][tt]])
                rms_rstd(rb, tt, dst=rstd_b[:, tt, :], dst_res=rstd_res[tt])
            w1 = dram["m_w1"]
            w2 = dram["m_w2"]
            state = {}
            rlc = [0]

            def W1(i):
                fb, tt = divmod(i, NT)
                if fb == 0:
                    prep(tt)
                if tt == 0:
                    s1 = load_w(w1[l, :, fb * 512:(fb + 1) * 512].rearrange("(kc p) f -> p kc f", p=128), (8, 512))
                    s2 = load_w(w2[l, fb * 512:(fb + 1) * 512, :].rearrange("(fc p) d -> p fc d", p=128), (4, 1024))
                    state[fb] = (s1, s2)
                (wa, war), _ = state[fb]
                ab = i % 2
                for fc in range(4):
                    bk, bres = next_bank()
                    fns = [I("matmul",
                        bk[:], lhsT=wa[:, kc, fc * 128:(fc + 1) * 128], rhs=hT[:, kc, tt * 512:(tt + 1) * 512],
                        start=(kc == 0), stop=(kc == KC - 1)) for kc in range(KC)]
                    sc.op("pe", fns, rd=[war] + [hT_res[kc][tt] for kc in range(KC)], wr=[bres])
                    r = rlc[0] % 2
                    rlc[0] += 1
                    sc.op("dve", I("scalar_tensor_tensor", out=rl[r][:], in0=bk[:], scalar=0.0, in1=rstd_b[:, tt, :],
                                   op0=ALU.max, op1=ALU.mult),
                          rd=[bres, rstd_res[tt]], wr=[rl_res[r]])
                    sc.op("act", I("activation", out=aT[ab][:, fc, :], in_=rl[r][:], func=AF.Square),
                          rd=[rl_res[r]], wr=[aT_res[ab][fc]])

            def W2(i):
                fb, tt = divmod(i, NT)
                _, (wb, wbr) = state[fb]
                ab = i % 2
                for dc in range(KC):
                    bk, bres = next_bank()
                    fns = [I("matmul",
                        bk[:], lhsT=wb[:, fc, dc * 128:(dc + 1) * 128], rhs=aT[ab][:, fc, :],
                        start=(fc == 0), stop=(fc == 3)) for fc in range(4)]
                    sc.op("pe", fns, rd=[wbr] + aT_res[ab], wr=[bres])
                    add_to_x(dc, tt, bk, bres)
                if tail is not None and fb == 7:
                    tail(tt)
                if next_rn0 is not None and fb == 7:
                    next_rn0(tt)

            n = 8 * NT
            W1(0)
            for i in range(n):
                if i + 1 < n:
                    W1(i + 1)
                W2(i)
            sc.end_phase()

    def mixer_a(l, ia, with_input=False, rn0_done=False):
        gcol = PCOLS[f"norm_mix{l}"]
        bucol = PCOLS[f"a_bu{ia}"]
        vgcol = PCOLS[f"a_vng{ia}"]
        w_in = dram["a_w_in"]
        w_out = dram["a_w_out"]
        with ExitStack() as ph:
            Ssb = sb("a_S", (128, ACJ, 128), F32, ph)
            S_res = Res("a_S")
            wmb = sb("a_wmb", (128, 8, 128), BF16, ph)
            wmb_res = Res("a_wmb")
            brow = sb("a_brow", (128, AW), BF16, ph)
            brow_res = Res("a_brow")
            sel1 = sb("a_sel1", (128, 128), BF16, ph)
            sel1_res = Res("a_sel1")
            asem = sc.dma_sem(f"a_pro{ia}_w")
            asem_b = sc.dma_sem(f"a_pro{ia}_b")
            asem_r = sc.dma_sem(f"a_pro{ia}_r")
            asem_l = sc.dma_sem(f"a_pro{ia}_l")
            def rn0():
                rmsnorm(rb_persist, 0, gcol, lambda kc: hTp[0][:, kc, :], lambda kc: hTp_res[0][kc])

            with ExitStack() as pro:
                def prologue():
                    wst = sb("a_wst", (128, 8, 128), F32, pro)
                    wst_res = Res("a_wst")
                    wmf2 = sb("a_wmf2", (128, 8, 128), F32, pro)
                    wmf2_res = Res("a_wmf2")
                    rrow = sb("a_rrow", (33, 1024), F32, pro)
                    rrow_res = Res("a_rrow")
                    lrow = sb("a_lrow", (33, AW), F32, pro)
                    lrow_res = Res("a_lrow")
                    bsrc = sb("a_bsrc", (1, AW), F32, pro)
                    bsrc_res = Res("a_bsrc")
                    sc.op("sp", I("dma_start", out=wst[:], in_=dram["a_w_s"][ia].rearrange("g t s -> t g s")),
                          wr=[wst_res], dma=asem)
                    sc.op("sp", I("dma_start", out=bsrc[:], in_=dram["a_b_in"][ia:ia + 1, AW:2 * AW]),
                          wr=[bsrc_res], dma=asem_b)
                    yield
                    for half in range(2):
                        bk, bres = next_bank()
                        fns = [I("transpose",
                            out=bk[:, j * 128:(j + 1) * 128], in_=wst[:, half * 4 + j, :], identity=ident[:]) for j in range(4)]
                        sc.op("pe", fns, rd=[wst_res, const_res], wr=[bres])
                        sc.op("dve", I("tensor_tensor",
                            out=wmf2[:, half * 4:half * 4 + 4, :], in0=bk[:].rearrange("p (a b) -> p a b", a=4),
                            in1=cmask[:].unsqueeze(1).broadcast_to([128, 4, 128]), op=ALU.mult),
                            rd=[bres, const_res], wr=[wmf2_res])
                    sc.op("act", I("activation", out=wmb[:], in_=wmf2[:], func=AF.Copy), rd=[wmf2_res], wr=[wmb_res])
                    sc.op("dve", I("memset", sel1[:], 0.0), wr=[sel1_res])
                    sc.op("dve", I("memset", sel1[0:1, :], 1.0), wr=[sel1_res])
                    sc.op("dve", I("memset", brow[:], 0.0), wr=[brow_res])
                    sc.op("dve", I("memset", rrow[:], 0.0), wr=[rrow_res])
                    sc.op("dve", I("memset", lrow[:], 0.0), wr=[lrow_res])
                    sc.op("dve", I("memset", lrow[32:33, :], 1.0), wr=[lrow_res])
                    sc.op("sp", I("dma_start", out=rrow[32:33, :], in_=dram["a_b_s"][ia:ia + 1, :]), wr=[rrow_res], dma=asem_r)
                    sc.op("sp", I("dma_start", out=lrow[0:1, :], in_=dram["a_vn_b"][ia:ia + 1, :]), wr=[lrow_res], dma=asem_l)
                    sc.op("dve", I("tensor_copy", out=brow[0:1, :], in_=bsrc[:]), rd=[bsrc_res], wr=[brow_res])
                    yield
                    for half in range(2):
                        bk, bres = next_bank()
                        sc.op("pe", I("matmul",
                            bk[0:1, :], lhsT=onesf[:, 0:1], rhs=wmf2[:, half * 4:half * 4 + 4, :], start=True, stop=True),
                            rd=[wmf2_res, const_res], wr=[bres])
                        sc.op("dve", I("tensor_copy", out=rrow[0:1, half * 512:(half + 1) * 512], in_=bk[0:1, :]),
                              rd=[bres], wr=[rrow_res])
                    yield
                    for q in range(ACJ // 4):
                        bk, bres = next_bank()
                        fns = []
                        for j in range(4):
                            cj = q * 4 + j
                            g = cj // 3
                            fns.append(I("matmul",
                                bk[:, j * 128:(j + 1) * 128], lhsT=lrow[:, cj * 128:(cj + 1) * 128], rhs=rrow[:, g * 128:(g + 1) * 128],
                                start=True, stop=True))
                        sc.op("pe", fns, rd=[lrow_res, rrow_res], wr=[bres])
                        evac_copy(Ssb[:, q * 4:q * 4 + 4, :], bk[:].rearrange("p (a b) -> p a b", a=4), rd=[bres], wr=[S_res])

                gen = prologue()
                if with_input:
                    def after_chunk(c):
                        if c in (1, 6, 10, 13):
                            next(gen, None)
                        if c == 3:
                            rn0()
                    input_phase(pro, after_chunk)
                    for _ in gen:
                        pass
                else:
                    if not rn0_done:
                        rn0()
                    for _ in gen:
                        pass
                sc.end_phase()
            rb = rb_persist
            hTs, hTs_res = hTp, hTp_res
            v = sb("a_v", (128, 4, AW), BF16, ph)
            v_res = [[Res(f"a_v{c}_{vp}") for vp in range(6)] for c in range(4)]
            uT = sb("a_uT", (128, ACJ, 512), BF16, ph)
            uT_res = [Res(f"a_uT{cj}") for cj in range(ACJ)]
            gt = [sb(f"a_gt{i}", (128, 512), F32, ph) for i in range(2)]
            gt_res = [Res(f"a_gt{i}") for i in range(2)]
            stats = sb("a_stats", (128, 4, 6, 6), F32, ph)
            stats_res = [[Res(f"a_st{c}_{vp}") for vp in range(6)] for c in range(4)]
            mv = sb("a_mv", (128, 4, 2), F32, ph)
            sdv = sb("a_sdv", (128, 4), F32, ph)
            nmr = sb("a_nmr", (128, 4), F32, ph)
            small_res = [Res(f"a_small{c}") for c in range(4)]

            def RN(tt):
                hT, hT_res = hTs[tt % 2], hTs_res[tt % 2]
                rmsnorm(rb, tt, gcol, lambda kc: hT[:, kc, :], lambda kc: hT_res[kc])

            def V(tt):
                hT, hT_res = hTs[tt % 2], hTs_res[tt % 2]
                for vp in range(6):
                    wv, wvr = load_w(w_in[ia, :, AW + vp * 512:AW + (vp + 1) * 512].rearrange("(kc p) f -> p kc f", p=128), (8, 512))
                    for c in range(4):
                        bk, bres = next_bank()
                        fns = [I("matmul",
                            bk[:], lhsT=hT[:, kc, c * 128:(c + 1) * 128], rhs=wv[:, kc, :], start=(kc == 0), stop=False)
                            for kc in range(KC)]
                        fns.append(I("matmul",
                            bk[:], lhsT=sel1[:], rhs=brow[:, vp * 512:(vp + 1) * 512], start=False, stop=True))
                        sc.op("pe", fns, rd=[wvr, brow_res, sel1_res] + hT_res, wr=[bres])
                        sc.op("act", I("activation",
                            out=v[:, c, vp * 512:(vp + 1) * 512], in_=bk[:], func=AF.Gelu_apprx_tanh),
                            rd=[bres], wr=[v_res[c][vp]])
                        sc.op("dve", I("bn_stats", out=stats[:, c, vp, :], in_=v[:, c, vp * 512:(vp + 1) * 512]),
                              rd=[v_res[c][vp]], wr=[stats_res[c][vp]])

            def ST(tt):
                for c in range(4):
                    sc.op("dve", I("bn_aggr", out=mv[:, c, :], in_=stats[:, c, :, :]),
                          rd=stats_res[c], wr=[small_res[c]])
                    sc.op("act", I("activation", out=sdv[:, c:c + 1], in_=mv[:, c, 1:2], func=AF.Sqrt,
                                   bias=epst[:, 0:1], scale=1.0),
                          rd=[const_res], wr=[small_res[c]])
                    sc.op("dve", I("reciprocal", out=sdv[:, c:c + 1], in_=sdv[:, c:c + 1]), wr=[small_res[c]])
                    sc.op("dve", I("scalar_tensor_tensor",
                        out=nmr[:, c:c + 1], in0=mv[:, c, 0:1], scalar=-1.0, in1=sdv[:, c:c + 1], op0=ALU.mult, op1=ALU.mult),
                        wr=[small_res[c]])
                    sc.op("act", I("activation", out=v[:, c, :], in_=v[:, c, :], func=AF.Identity,
                                   bias=nmr[:, c:c + 1], scale=sdv[:, c:c + 1]),
                          rd=[small_res[c]], wr=v_res[c])

            def U_group(tt, cj, wu, wur):
                hT, hT_res = hTs[tt % 2], hTs_res[tt % 2]
                cc = cj % 4
                bk, bres = next_bank()
                fns = [I("matmul",
                    bk[:], lhsT=wu[:, kc, cc * 128:(cc + 1) * 128], rhs=hT[:, kc, :], start=(kc == 0), stop=(kc == KC - 1))
                    for kc in range(KC)]
                sc.op("pe", fns, rd=[wur] + hT_res, wr=[bres])
                sc.op("act", I("activation",
                    out=uT[:, cj, :], in_=bk[:], func=AF.Gelu_apprx_tanh, bias=pv[:, bucol + cj:bucol + cj + 1], scale=1.0),
                    rd=[bres, pv_res], wr=[uT_res[cj]])

            def SP_group(cj):
                g = cj // 3
                bk, bres = next_bank()
                fns = [I("matmul",
                    bk[:, c * 128:(c + 1) * 128], lhsT=v[:, c, cj * 128:(cj + 1) * 128], rhs=wmb[:, g, :], start=True, stop=True)
                    for c in range(4)]
                sc.op("pe", fns, rd=[wmb_res] + [v_res[c][cj // 4] for c in range(4)], wr=[bres])
                gi = cj % 2
                sc.op("dve", I("scalar_tensor_tensor",
                    out=gt[gi][:].rearrange("p (a b) -> p a b", a=4), in0=bk[:].rearrange("p (a b) -> p a b", a=4),
                    scalar=pv[:, vgcol + cj:vgcol + cj + 1],
                    in1=Ssb[:, cj:cj + 1, :].broadcast_to([128, 4, 128]), op0=ALU.mult, op1=ALU.add),
                    rd=[bres, S_res, pv_res], wr=[gt_res[gi]])
                sc.op("dve", I("tensor_tensor", out=uT[:, cj, :], in0=gt[gi][:], in1=uT[:, cj, :], op=ALU.mult),
                      rd=[gt_res[gi]], wr=[uT_res[cj]])

            def USP(tt):
                lag = 8 if tt == 0 else 2
                wu = wur = None
                for cj in range(ACJ):
                    if cj % 4 == 0:
                        up = cj // 4
                        wu, wur = load_w(w_in[ia, :, up * 512:(up + 1) * 512].rearrange("(kc p) f -> p kc f", p=128), (8, 512))
                    U_group(tt, cj, wu, wur)
                    if cj >= lag:
                        SP_group(cj - lag)
                for cj in range(ACJ - lag, ACJ):
                    SP_group(cj)

            def WO(tt):
                for dcp in range(4):
                    sl = []
                    for h in range(2):
                        sl.append(load_w(w_out[ia, h * 1536:(h + 1) * 1536, dcp * 256:(dcp + 1) * 256].rearrange(
                            "(cj p) d -> p cj d", p=128), (12, 256)))
                    for dci in range(2):
                        dc = dcp * 2 + dci
                        bk, bres = next_bank()
                        fns = [I("matmul",
                            bk[:], lhsT=sl[cj // 12][0][:, cj % 12, dci * 128:(dci + 1) * 128], rhs=uT[:, cj, :],
                            start=(cj == 0), stop=(cj == ACJ - 1)) for cj in range(ACJ)]
                        sc.op("pe", fns, rd=[sl[0][1], sl[1][1]] + uT_res, wr=[bres])
                        add_to_x(dc, tt, bk, bres)

            if NT > 1 and rn0_done < 2:
                RN(1)
            V(0)
            ST(0)
            for tt in range(NT):
                USP(tt)
                if tt + 2 < NT:
                    RN(tt + 2)
                if tt + 1 < NT:
                    V(tt + 1)
                    ST(tt + 1)
                WO(tt)
            sc.end_phase()

    def mixer_b(l, rn0_done=False):
        gcol = PCOLS[f"norm_mix{l}"]
        bgcol = PCOLS["b_bgrp"]
        bscol = PCOLS["b_scale"]
        W = 528
        with ExitStack() as ph:
            rb = rb_persist
            hT, hT_res = hTp[0], hTp_res[0]
            pexts = [sb(f"b_pext{i}", (128, KC, W), F32, ph) for i in range(2)]
            pm_ress = [[Res(f"b_pm{i}_{kc}") for kc in range(KC)] for i in range(2)]
            ph_ress = [Res(f"b_phalo{i}") for i in range(2)]
            tA = sb("b_tA", (128, KC, W), F32, ph)
            tB = sb("b_tB", (128, 6, W), F32, ph)
            tA_res = Res("b_tA")
            tB_res = Res("b_tB")
            pooled = sb("b_pooled", (128, KC, 512), BF16, ph)
            pooled_res = [Res(f"b_pool{g}") for g in range(4)]
            y2 = pooled
            bsp = sb("b_bsp", (128, 8), F32, ph)
            bsp_res = Res("b_bsp")
            fx = sb("b_fx", (128, 2, 16), F32, ph)
            fx_res = Res("b_fx")
            sc.op("dve", I("tensor_tensor", out=bsp[:], in0=pv[:, bgcol:bgcol + 8], in1=pv[:, bscol:bscol + 8], op=ALU.mult),
                  rd=[pv_res], wr=[bsp_res])

            def RNb(tt):
                hT, hT_res = hTp[tt % 2], hTp_res[tt % 2]
                rmsnorm(rb, tt, gcol, lambda kc: hT[:, kc, :], lambda kc: hT_res[kc])

            def S1(tt):
                hT, hT_res = hTp[tt % 2], hTp_res[tt % 2]
                pext, pm_res, ph_res = pexts[tt % 2], pm_ress[tt % 2], ph_ress[tt % 2]
                if tt == 0:
                    sc.op("dve", I("memset", pext[:, :, 0:16], 0.0), wr=[ph_res])
                else:
                    sc.op("dve", I("tensor_copy", out=pext[:, :, 0:16], in_=pexts[(tt - 1) % 2][:, :, 512:528]),
                          rd=pm_ress[(tt - 1) % 2], wr=[ph_res])
                win = [load_w(dram["b_w_in"][0, :, h * 512:(h + 1) * 512].rearrange("(kc p) f -> p kc f", p=128), (8, 512))
                       for h in range(2)]
                for dc in range(KC):
                    bk, bres = next_bank()
                    wt, wtr = win[dc // 4]
                    fns = [I("matmul",
                        bk[:], lhsT=wt[:, kc, (dc % 4) * 128:(dc % 4 + 1) * 128], rhs=hT[:, kc, :], start=(kc == 0), stop=(kc == KC - 1))
                        for kc in range(KC)]
                    sc.op("pe", fns, rd=[wtr] + hT_res, wr=[bres])
                    sc.op("act", I("activation", out=pext[:, dc, 16:528], in_=bk[:], func=AF.Copy), rd=[bres], wr=[pm_res[dc]])

            def S2a(tt):
                pext, pm_res, ph_res = pexts[tt % 2], pm_ress[tt % 2], ph_ress[tt % 2]
                allp = pm_res + [ph_res]
                sc.op("dve", I("tensor_tensor", out=tA[:, :, 1:W], in0=pext[:, :, 1:W], in1=pext[:, :, 0:W - 1], op=ALU.add),
                      rd=allp, wr=[tA_res])
                sc.op("dve", I("tensor_tensor", out=tB[:, 0:6, 3:W], in0=tA[:, 2:8, 3:W], in1=tA[:, 2:8, 1:W - 2], op=ALU.add),
                      rd=[tA_res], wr=[tB_res])
                sc.op("dve", I("tensor_tensor", out=tA[:, 4:8, 7:W], in0=tB[:, 2:6, 7:W], in1=tB[:, 2:6, 3:W - 4], op=ALU.add),
                      rd=[tB_res], wr=[tA_res])
                sc.op("dve", I("tensor_tensor", out=tB[:, 4:6, 15:W], in0=tA[:, 6:8, 15:W], in1=tA[:, 6:8, 7:W - 8], op=ALU.add),
                      rd=[tA_res], wr=[tB_res])
                srcs = [tA[:, 0:2, :], tB[:, 0:2, :], tA[:, 4:6, :], tB[:, 4:6, :]]
                for g in range(4):
                    wn = B_WINDOWS[g]
                    src = srcs[g]
                    sc.op("dve", I("scalar_tensor_tensor",
                        out=pooled[:, 2 * g:2 * g + 2, :], in0=src[:, :, 16:W], scalar=1.0 / wn,
                        in1=pext[:, 2 * g:2 * g + 2, 16:W], op0=ALU.mult, op1=ALU.subtract),
                        rd=[tA_res, tB_res] + allp, wr=[pooled_res[g]])
                    if tt == 0:
                        n = wn - 1
                        sc.op("dve", I("tensor_tensor", out=fx[:, :, 0:n], in0=src[:, :, 16:16 + n],
                                       in1=invc[:, 0:n].unsqueeze(1).broadcast_to([128, 2, n]), op=ALU.mult),
                              rd=[tA_res, tB_res, invc_res], wr=[fx_res])
                        sc.op("dve", I("tensor_tensor", out=pooled[:, 2 * g:2 * g + 2, 0:n], in0=fx[:, :, 0:n],
                                       in1=pext[:, 2 * g:2 * g + 2, 16:16 + n], op=ALU.subtract),
                              rd=[fx_res] + allp, wr=[pooled_res[g]])

            def S2b(tt):
                wg, wgr = load_w(dram["b_w_grp"][0].rearrange("g (kk p) e -> p (g kk) e", p=128), (8, 256))
                for g in range(4):
                    grp_banks = []
                    for ec in range(2):
                        bk, bres = next_bank()
                        fns = [I("matmul",
                            bk[:], lhsT=wg[:, g * 2 + kk, ec * 128:(ec + 1) * 128], rhs=pooled[:, 2 * g + kk, :],
                            start=(kk == 0), stop=(kk == 1)) for kk in range(2)]
                        sc.op("pe", fns, rd=[wgr, pooled_res[g]], wr=[bres])
                        grp_banks.append((bk, bres))
                    for ec in range(2):
                        oc = 2 * g + ec
                        bk, bres = grp_banks[ec]
                        sc.op("act", I("activation", out=y2[:, oc, :], in_=bk[:], func=AF.Identity,
                                       bias=bsp[:, oc:oc + 1], scale=pv[:, bscol + oc:bscol + oc + 1]),
                              rd=[bres, pv_res, bsp_res], wr=[pooled_res[g]])
                wo = [load_w(dram["b_w_out"][0, :, h * 512:(h + 1) * 512].rearrange("(kc p) f -> p kc f", p=128), (8, 512))
                      for h in range(2)]
                for dc in range(KC):
                    bk, bres = next_bank()
                    wt, wtr = wo[dc // 4]
                    fns = [I("matmul",
                        bk[:], lhsT=wt[:, kc, (dc % 4) * 128:(dc % 4 + 1) * 128], rhs=y2[:, kc, :], start=(kc == 0), stop=(kc == KC - 1))
                        for kc in range(KC)]
                    sc.op("pe", fns, rd=[wtr] + pooled_res, wr=[bres])
                    add_to_x(dc, tt, bk, bres)

            if not rn0_done:
                RNb(0)
            S1(0)
            if NT > 1 and rn0_done < 2:
                RNb(1)
            for tt in range(NT):
                S2a(tt)
                if tt + 1 < NT:
                    S1(tt + 1)
                if tt + 2 < NT:
                    RNb(tt + 2)
                S2b(tt)
            sc.end_phase()

    def mixer_c(l, rn0_done=False):
        gcol = PCOLS[f"norm_mix{l}"]
        wcol = PCOLS["c_wdw"]
        bdcol = PCOLS["c_bdw"]
        lgcol = PCOLS["c_lng"]
        lbcol = PCOLS["c_lnb"]
        W = 544
        with ExitStack() as ph:
            rb = rb_persist
            hT, hT_res = hTp[0], hTp_res[0]
            identb = sb("c_identb", (128, 128), BF16, ph)
            identb_res = Res("c_identb")
            yext = sb("c_yext", (128, KC, W), BF16, ph)
            ym_res = [Res(f"c_ym{kc}") for kc in range(KC)]
            yh_res = Res("c_yhalo")
            dg = [sb(f"c_dg{i}", (128, CK, 128), BF16, ph) for i in range(2)]
            dg_res = [Res(f"c_dg{i}") for i in range(2)]
            accs = [sb(f"c_acc{i}", (128, KC, 512), F32, ph) for i in range(2)]
            accs_res = [[Res(f"c_acc{i}_{kc}") for kc in range(KC)] for i in range(2)]
            msq = sb("c_msq", (128, 512), F32, ph)
            msq_res = Res("c_msq")
            sg = [sb(f"c_sg{i}", (128, 512), F32, ph) for i in range(2)]
            sg_res = [Res(f"c_sg{i}") for i in range(2)]
            mean = sb("c_mean", (128, 512), F32, ph)
            rstd = sb("c_rstd", (128, 512), F32, ph)
            mean_res = Res("c_mean")
            rstd_res = Res("c_rstd")
            nr = [sb(f"c_nr{i}", (128, 512), F32, ph) for i in range(2)]
            nr_res = [Res(f"c_nr{i}") for i in range(2)]
            z, z_res = hTp[1], hTp_res[1]
            sc.op("dve", I("tensor_copy", out=identb[:], in_=ident[:]), rd=[const_res], wr=[identb_res])

            def RNc(tt):
                rmsnorm(rb, tt, gcol, lambda kc: hT[:, kc, :], lambda kc: hT_res[kc])

            def frontA(tt):
                if tt == 0:
                    sc.op("dve", I("memset", yext[:, :, 0:32], 0.0), wr=[yh_res])
                else:
                    sc.op("dve", I("tensor_copy", out=yext[:, :, 0:32], in_=yext[:, :, 512:544]), rd=ym_res, wr=[yh_res])
                wa = [load_w(dram["c_w_in"][0, :, h * 512:(h + 1) * 512].rearrange("(kc p) f -> p kc f", p=128), (8, 512))
                      for h in range(2)]
                wgt = [load_w(dram["c_w_in"][0, :, D + h * 512:D + (h + 1) * 512].rearrange("(kc p) f -> p kc f", p=128), (8, 512))
                       for h in range(2)]
                for dc in range(KC):
                    bka, bresa = next_bank()
                    bkg, bresg = next_bank()
                    for (bk, bres, wsl) in ((bka, bresa, wa), (bkg, bresg, wgt)):
                        wt, wtr = wsl[dc // 4]
                        fns = [I("matmul",
                            bk[:], lhsT=wt[:, kc, (dc % 4) * 128:(dc % 4 + 1) * 128], rhs=hT[:, kc, :], start=(kc == 0), stop=(kc == KC - 1))
                            for kc in range(KC)]
                        sc.op("pe", fns, rd=[wtr] + hT_res, wr=[bres])
                    si = dc % 2
                    sc.op("act", I("activation", out=sg[si][:], in_=bkg[:], func=AF.Sigmoid),
                          rd=[bresg], wr=[sg_res[si]])
                    sc.op("dve", I("tensor_tensor", out=yext[:, dc, 32:W], in0=bka[:], in1=sg[si][:], op=ALU.mult),
                          rd=[bresa, sg_res[si]], wr=[ym_res[dc]])

            def build(dc):
                di = dc % 2
                col = wcol + dc * CK
                sc.op("dve", I("tensor_tensor", out=dg[di][:],
                               in0=identb[:].unsqueeze(1).broadcast_to([128, CK, 128]),
                               in1=pv[:, col:col + CK].unsqueeze(2).broadcast_to([128, CK, 128]), op=ALU.mult),
                      rd=[identb_res, pv_res], wr=[dg_res[di]])

            def prebuild(tt):
                build(0)
                build(1)

            def frontB(tt, ln_tt=None):
                ac, ac_res = accs[tt % 2], accs_res[tt % 2]
                for dc in range(KC):
                    di = dc % 2
                    bk, bres = next_bank()
                    fns = [I("matmul", bk[:], lhsT=dg[di][:, k, :], rhs=yext[:, dc, 2 + k:2 + k + 512], start=(k == 0), stop=(k == CK - 1))
                           for k in range(CK)]
                    sc.op("pe", fns, rd=[dg_res[di], ym_res[dc], yh_res], wr=[bres])
                    sc.op("act", I("activation", out=ac[:, dc, :], in_=bk[:], func=AF.Identity,
                                   bias=pv[:, bdcol + dc:bdcol + dc + 1], scale=1.0),
                          rd=[bres, pv_res], wr=[ac_res[dc]])
                    if dc + 2 < KC:
                        build(dc + 2)
                    if ln_tt is not None:
                        if dc == 0:
                            LN_head(ln_tt)
                        LN_chunk(ln_tt, dc)
                bkm, bresm = next_bank()
                bkq, bresq = next_bank()
                for dc in range(KC):
                    y0, y0r = next_sq()
                    sc.op("act", I("activation", out=y0[:], in_=ac[:, dc, :], func=AF.Copy),
                          rd=[ac_res[dc]], wr=[y0r])
                    sc.op("pe", I("matmul", bkm[:], lhsT=onesm[:], rhs=y0[:], start=(dc == 0), stop=(dc == KC - 1)),
                          rd=[y0r, const_res], wr=[bresm] if dc == 0 else [])
                    y1, y1r = next_sq()
                    sc.op("act", I("activation", out=y1[:], in_=ac[:, dc, :], func=AF.Square),
                          rd=[ac_res[dc]], wr=[y1r])
                    sc.op("pe", I("matmul", bkq[:], lhsT=onesm[:], rhs=y1[:], start=(dc == 0), stop=(dc == KC - 1)),
                          rd=[y1r, const_res], wr=[bresq] if dc == 0 else [])
                bresm.w = {"E_pe": sc.cnt["E_pe"]}
                bresq.w = {"E_pe": sc.cnt["E_pe"]}
                sc.op("act", I("activation", out=mean[:], in_=bkm[:], func=AF.Copy), rd=[bresm], wr=[mean_res])
                sc.op("dve", I("tensor_copy", out=rstd[:], in_=bkq[:]), rd=[bresq], wr=[rstd_res])

            def LN_head(tt):
                sc.op("dve", I("tensor_tensor", out=msq[:], in0=mean[:], in1=mean[:], op=ALU.mult), rd=[mean_res], wr=[msq_res])
                sc.op("dve", I("tensor_tensor", out=rstd[:], in0=rstd[:], in1=msq[:], op=ALU.subtract), rd=[msq_res], wr=[rstd_res])
                sc.op("act", I("activation", out=rstd[:], in_=rstd[:], func=AF.Sqrt, bias=epst[:, 0:1], scale=1.0),
                      rd=[const_res], wr=[rstd_res])
                sc.op("dve", I("reciprocal", out=rstd[:], in_=rstd[:]), wr=[rstd_res])

            def zbuf(tt):
                if NT > 1 and tt == NT - 1:
                    return hTp[0], hTp_res[0]
                return hTp[1], hTp_res[1]

            def LN_chunk(tt, dc):
                z, z_res = zbuf(tt)
                ac, ac_res = accs[tt % 2], accs_res[tt % 2]
                ni = dc % 2
                sc.op("dve", I("tensor_tensor", out=nr[ni][:], in0=ac[:, dc, :], in1=mean[:], op=ALU.subtract),
                      rd=[ac_res[dc], mean_res], wr=[nr_res[ni]])
                sc.op("dve", I("tensor_tensor", out=nr[ni][:], in0=nr[ni][:], in1=rstd[:], op=ALU.mult),
                      rd=[rstd_res], wr=[nr_res[ni]])
                sc.op("act", I("activation",
                    out=z[:, dc, :], in_=nr[ni][:], func=AF.Silu, bias=pv[:, lbcol + dc:lbcol + dc + 1],
                    scale=pv[:, lgcol + dc:lgcol + dc + 1]), rd=[nr_res[ni], pv_res], wr=[z_res[dc]])

            def backPE(tt):
                z, z_res = zbuf(tt)
                wo = [load_w(dram["c_w_out"][0, :, h * 512:(h + 1) * 512].rearrange("(kc p) f -> p kc f", p=128), (8, 512))
                      for h in range(2)]
                for dc in range(KC):
                    bk, bres = next_bank()
                    wt, wtr = wo[dc // 4]
                    fns = [I("matmul",
                        bk[:], lhsT=wt[:, kc, (dc % 4) * 128:(dc % 4 + 1) * 128], rhs=z[:, kc, :], start=(kc == 0), stop=(kc == KC - 1))
                        for kc in range(KC)]
                    sc.op("pe", fns, rd=[wtr] + z_res, wr=[bres])
                    add_to_x(dc, tt, bk, bres)

            if not rn0_done:
                RNc(0)
            prebuild(0)
            frontA(0)
            if NT > 1:
                RNc(1)
            frontB(0)
            for tt in range(NT):
                if tt + 1 < NT:
                    prebuild(tt + 1)
                    frontA(tt + 1)
                    if tt + 2 < NT:
                        RNc(tt + 2)
                    frontB(tt + 1, ln_tt=tt)
                    if tt + 2 == NT:
                        LN_head(tt + 1)
                        for dc in range(KC):
                            LN_chunk(tt + 1, dc)
                elif NT == 1:
                    LN_head(tt)
                    for dc in range(KC):
                        LN_chunk(tt, dc)
                backPE(tt)
            sc.end_phase()

    NOB = 2
    osem = [sc.dma_sem(f"xout{i}") for i in range(NOB)]

    def out_setup(stack):
        ost = [sb(f"ost{i}", (128, D), F32, stack) for i in range(NOB)]
        ost_res = [Res(f"ost{i}") for i in range(NOB)]
        if final:
            gbc = sb("f_gbc", (128, D), F32, stack)
            gbc_res = Res("f_gbc")
            gsem = sc.dma_sem("f_gbc")
            junk = [sb(f"f_junk{i}", (128, 512), F32, stack) for i in range(2)]
            junk_res = [Res(f"f_junk{i}") for i in range(2)]
            ss = sb("f_ss", (128, NOB, 4), F32, stack)
            ss_res = [Res(f"f_ss{i}") for i in range(NOB)]
            sc.op("sp", I("dma_start", out=gbc[:], in_=dram["norm_final"].partition_broadcast(128)), wr=[gbc_res], dma=gsem)

        def tail(tt):
            for ci in range(4):
                c = tt * 4 + ci
                ob = c % NOB
                bks = []
                for half in range(2):
                    bk, bres = next_bank()
                    fns = []
                    rds = [const_res]
                    for j in range(4):
                        dc = half * 4 + j
                        rds.append(xT_res[dc][tt])
                        fns.append(I("transpose", out=bk[:, j * 128:(j + 1) * 128], in_=xT[:, dc, c * 128:(c + 1) * 128],
                                     identity=ident[:]))
                    sc.op("pe", fns, rd=rds, wr=[bres])
                    bks.append((bk, bres))
                if final:
                    for half in range(2):
                        bk, bres = bks[half]
                        sc.op("act", I("activation", out=junk[half][:], in_=bk[:], func=AF.Square,
                                       accum_out=ss[:, ob, half:half + 1]),
                              rd=[bres], wr=[junk_res[half], ss_res[ob]])
                    sc.op("dve", I("tensor_tensor", out=ss[:, ob, 2:3], in0=ss[:, ob, 0:1], in1=ss[:, ob, 1:2], op=ALU.add),
                          wr=[ss_res[ob]])
                    sc.op("act", I("activation", out=ss[:, ob, 3:4], in_=ss[:, ob, 2:3], func=AF.Sqrt,
                                   bias=epst[:, 0:1], scale=1.0 / D), rd=[const_res], wr=[ss_res[ob]])
                    sc.op("dve", I("reciprocal", out=ss[:, ob, 3:4], in_=ss[:, ob, 3:4]), wr=[ss_res[ob]])
                    for half in range(2):
                        bk, bres = bks[half]
                        sc.op("dve", I("scalar_tensor_tensor", out=ost[ob][:, half * 512:(half + 1) * 512], in0=bk[:],
                                       scalar=ss[:, ob, 3:4], in1=gbc[:, half * 512:(half + 1) * 512],
                                       op0=ALU.mult, op1=ALU.mult),
                              rd=[bres, ss_res[ob], gbc_res], wr=[ost_res[ob]])
                else:
                    for half in range(2):
                        bk, bres = bks[half]
                        evac_copy(ost[ob][:, half * 512:(half + 1) * 512], bk[:], rd=[bres], wr=[ost_res[ob]])
                sc.op("sp", I("dma_start", out=out_d[c * 128:(c + 1) * 128, :], in_=ost[ob][:]),
                      rd=[ost_res[ob]], dma=osem[ob])
        return tail

    first_is_a = bool(layers) and layers[0] % 3 == 0
    if not first_is_a:
        with ExitStack() as ph:
            input_phase(ph)
            sc.end_phase()
    def n_pre(lnext):
        return 1 if (lnext % 3 == 2 or NT < 2) else 2

    def make_rn0(lnext):
        g = PCOLS[f"norm_mix{lnext}"]
        npre = n_pre(lnext)

        def f(tt):
            if tt < npre:
                b = tt % 2
                rmsnorm(rb_persist, tt, g, lambda kc: hTp[b][:, kc, :], lambda kc: hTp_res[b][kc])
        return f

    for li, l in enumerate(layers):
        kind = l % 3
        pre = n_pre(l) if li > 0 else 0
        if kind == 0:
            mixer_a(l, l // 3, with_input=(li == 0), rn0_done=pre)
        elif kind == 1:
            mixer_b(l, rn0_done=pre)
        else:
            mixer_c(l, rn0_done=pre)
        last = li == len(layers) - 1
        mlp(l, tail_setup=out_setup if last else None, next_rn0=None if last else make_rn0(layers[li + 1]))
    if not layers:
        with ExitStack() as ph:
            tail = out_setup(ph)
            for tt in range(NT):
                tail(tt)
    sc.wait_all("sp", osem)
    sc.wait_all("sp", ["E_pe", "E_act", "E_dve"] + ring_sem)

    with nc.Block() as block:
        sc.emit(block)
    es.close()
    return nc


S_FULL = 2048
N_CORES = 8
_WEIGHT_KEYS = {
    0: ("a_w_in", "a_b_in", "a_vn_b", "a_w_s", "a_b_s", "a_w_out"),
    1: ("b_w_in", "b_w_grp", "b_w_out"),
    2: ("c_w_in", "c_w_out"),
}


def _in_maps(inp, x_shards, pvec, layers, final=True):
    kinds = sorted(set(l % 3 for l in layers))
    shared = {"pvec": pvec}
    if final:
        shared["norm_final"] = np.ascontiguousarray(np.asarray(inp["norm_final"], dtype=np.float32))
    for k in kinds:
        for name in _WEIGHT_KEYS[k]:
            a = np.ascontiguousarray(np.asarray(inp[name], dtype=np.float32))
            if name == "a_b_s":
                a = a.reshape(2, 8 * 128)
            shared[name] = a
    if layers:
        shared["m_w1"] = np.ascontiguousarray(np.asarray(inp["m_w1"], dtype=np.float32))
        shared["m_w2"] = np.ascontiguousarray(np.asarray(inp["m_w2"], dtype=np.float32))
    maps = []
    for xs in x_shards:
        m = dict(shared)
        m["x"] = np.ascontiguousarray(xs)
        maps.append(m)
    return maps


LAUNCH_PLAN = [([0, 1, 2, 3], True)]


def kernel(**inputs):
    x = np.asarray(inputs["x"], dtype=np.float32)
    pvec = build_pvec(inputs)
    cur = [x[i] for i in range(N_CORES)]
    for layers, final in LAUNCH_PLAN:
        nc = build_program(S_FULL, layers, final)
        maps = _in_maps(inputs, cur, pvec, layers, final)
        res = run_bass_kernel_spmd(nc, maps, core_ids=list(range(N_CORES)))
        cur = [np.asarray(r["out"], dtype=np.float32) for r in res.results]
    return np.stack(cur, axis=0)
```
